# Optimizing a Trainium2 kernel written in Bass

```python
import math
import jax
import jax.numpy as jnp
from jax import lax
import numpy as np

D_MODEL = 1024
BATCH = 4
SEQ = 8192
DEPTH = 1

CTX_LEN = 256
GRID_W = 64
H_A = 4
DK_A = 128
DV_A = 128
W_A = H_A * DV_A
QKV_A = 2 * H_A * DK_A + W_A
CONV_K = 5
H_B = 4
DK_B = 128
DV_B = 128
W_B = H_B * DV_B
CHUNK = 64
D_FF = ((8 * D_MODEL + 3 * 256 - 1) // (3 * 256)) * 256
EPS = 1e-6
IN_SIZES = (QKV_A, H_A, H_A, H_A, H_A, W_A, H_B * DK_B, W_B, H_B * DK_B, H_B * DK_B, W_B, D_MODEL, D_MODEL)
D_IN = sum(IN_SIZES)

kernel_name = "hybrid_gdn_hgrn2_prefix_dit_block"


def split_last(z, sizes):
    idx = np.cumsum(sizes)[:-1].tolist()
    return jnp.split(z, idx, axis=-1)


def rms_norm(x, g):
    xf = x.astype(jnp.float32)
    y = xf * lax.rsqrt(jnp.mean(xf * xf, axis=-1, keepdims=True) + EPS)
    return (y * g.astype(jnp.float32)).astype(x.dtype)


def l2_normalize(t):
    tf = t.astype(jnp.float32)
    return tf * lax.rsqrt(jnp.sum(tf * tf, axis=-1, keepdims=True) + EPS)


def to_heads(t, h):
    b, l, _ = t.shape
    return t.reshape(b, l, h, -1).transpose(0, 2, 1, 3)


def from_heads(t):
    b, h, l, d = t.shape
    return t.transpose(0, 2, 1, 3).reshape(b, l, h * d)


def short_conv(t, w):
    pad = CONV_K // 2
    return lax.conv_general_dilated(t, w[:, None, :].astype(t.dtype), (1,), [(pad, pad)],
                                    dimension_numbers=('NWC', 'WIO', 'NWC'),
                                    feature_group_count=t.shape[-1])


def _chunks(t):
    b, h, l = t.shape[:3]
    return t.astype(jnp.float32).reshape((b, h, l // CHUNK, CHUNK) + t.shape[3:])


def _unchunk(o):
    n, b, h, c, d = o.shape
    return jnp.moveaxis(o, 0, 2).reshape(b, h, n * c, d)


def gdn_chunk_scan(q, k, v, log_alpha, beta, s0, with_output):
    q, k, v, log_alpha, beta = map(_chunks, (q, k, v, log_alpha, beta))
    dk = k.shape[-1]
    g = jnp.cumsum(log_alpha, axis=-1)
    incl = jnp.tril(jnp.ones((CHUNK, CHUNK), dtype=bool))
    strict = jnp.tril(jnp.ones((CHUNK, CHUNK), dtype=bool), -1)
    decay = jnp.exp(jnp.where(incl, g[..., :, None] - g[..., None, :], -jnp.inf))
    a_mat = jnp.where(strict, beta[..., :, None] * jnp.einsum('bhnid,bhnjd->bhnij', k, k) * decay, 0.0)
    rhs = jnp.concatenate([(beta * jnp.exp(g))[..., None] * k, beta[..., None] * v], axis=-1)
    sol = lax.linalg.triangular_solve(a_mat, rhs, left_side=True, lower=True, unit_diagonal=True)
    w, u = sol[..., :dk], sol[..., dk:]
    g_end = g[..., -1]
    k_end = k * jnp.exp(g_end[..., None] - g)[..., None]
    xs = [w, u, k_end, jnp.exp(g_end)]
    if with_output:
        xs += [q * jnp.exp(g)[..., None], jnp.einsum('bhnid,bhnjd->bhnij', q, k) * decay]
    xs = [jnp.moveaxis(t, 2, 0) for t in xs]

    def step(s, xc):
        w_c, u_c, k_end_c, gam_end_c = xc[:4]
        u_c = u_c - jnp.einsum('bhck,bhkv->bhcv', w_c, s)
        s_next = gam_end_c[..., None, None] * s + jnp.einsum('bhck,bhcv->bhkv', k_end_c, u_c)
        if not with_output:
            return s_next, None
        qg_c, att_c = xc[4:]
        o = jnp.einsum('bhck,bhkv->bhcv', qg_c, s) + jnp.einsum('bhij,bhjv->bhiv', att_c, u_c)
        return s_next, o

    s_final, o = lax.scan(step, s0, xs)
    return (_unchunk(o) if with_output else None), s_final


def hgrn2_chunk_scan(q, k, v, log_f, s0, with_output):
    q, k, v, log_f = map(_chunks, (q, k, v, log_f))
    lg = jnp.cumsum(log_f, axis=3)
    lg_end = lg[:, :, :, -1]
    k_end = k * jnp.exp(lg_end[:, :, :, None] - lg)
    xs = [k_end, v, jnp.exp(lg_end)]
    if with_output:
        xs += [q, k, lg]
    xs = [jnp.moveaxis(t, 2, 0) for t in xs]
    incl = jnp.tril(jnp.ones((CHUNK, CHUNK), dtype=bool))

    def step(s, xc):
        k_end_c, v_c, f_end_c = xc[:3]
        s_next = f_end_c[..., None] * s + jnp.einsum('bhck,bhcv->bhkv', k_end_c, v_c)
        if not with_output:
            return s_next, None
        q_c, k_c, lg_c = xc[3:]
        pair = jnp.exp(jnp.where(incl[:, :, None], lg_c[:, :, :, None, :] - lg_c[:, :, None, :, :], -jnp.inf))
        att = jnp.einsum('bhik,bhjk,bhijk->bhij', q_c, k_c, pair)
        o = jnp.einsum('bhck,bhkv->bhcv', q_c * jnp.exp(lg_c), s) + jnp.einsum('bhij,bhjv->bhiv', att, v_c)
        return s_next, o

    s_final, o = lax.scan(step, s0, xs)
    return (_unchunk(o) if with_output else None), s_final


def run_bidirectional(scan_fn, ctx_dirs, lat_dirs, s0, need_ctx_out):
    o_lat, o_ctx = None, None
    for direction in range(2):
        rev = (lambda t: jnp.flip(t, axis=2)) if direction == 1 else (lambda t: t)
        oc, s_ctx = scan_fn(*[rev(t) for t in ctx_dirs[direction]], s0, need_ctx_out)
        ol, _ = scan_fn(*[rev(t) for t in lat_dirs[direction]], s_ctx, True)
        o_lat = rev(ol) if o_lat is None else o_lat + rev(ol)
        if need_ctx_out:
            o_ctx = rev(oc) if o_ctx is None else o_ctx + rev(oc)
    return o_lat, o_ctx


def mixer_inputs(h, w_in, conv_w, a_log, dt_bias, lb, rows):
    f32 = jnp.float32
    (qkv, a_f, a_b, be_f, be_b, g_a, q_b, i_b, f_f, f_b, g_b, m_a, m_b) = split_last(h @ w_in, IN_SIZES)
    bsz, l, ch = qkv.shape
    if rows is not None:
        qkv = short_conv(qkv.reshape(bsz * rows, GRID_W, ch), conv_w).reshape(bsz, l, ch)
    else:
        qkv = short_conv(qkv, conv_w)
    q_a, k_a, v_a = split_last(jax.nn.silu(qkv), (H_A * DK_A, H_A * DK_A, W_A))
    q_a = l2_normalize(to_heads(q_a, H_A)) * (DK_A ** -0.5)
    k_a = l2_normalize(to_heads(k_a, H_A))
    v_a = to_heads(v_a, H_A).astype(f32)
    gdn = []
    for d, (a_raw, be_raw) in enumerate(((a_f, be_f), (a_b, be_b))):
        a_raw = jnp.swapaxes(a_raw.astype(f32), 1, 2)
        log_alpha = -jnp.exp(a_log[d].astype(f32))[None, :, None] * jax.nn.softplus(
            a_raw + dt_bias[d].astype(f32)[None, :, None])
        beta = jax.nn.sigmoid(jnp.swapaxes(be_raw.astype(f32), 1, 2))
        gdn.append((q_a, k_a, v_a, log_alpha, beta))
    lb = lb.reshape(1, H_B, 1, DK_B)
    log_lb, log_1m_lb = jnp.log(lb), jnp.log1p(-lb)
    q_b = to_heads(jax.nn.silu(q_b), H_B).astype(f32)
    v_b = to_heads(i_b, H_B).astype(f32)
    hgrn = []
    for f_raw in (f_f, f_b):
        zf = to_heads(f_raw, H_B).astype(f32)
        log_f = jnp.logaddexp(log_lb, log_1m_lb + jax.nn.log_sigmoid(zf))
        k_b = jnp.exp(log_1m_lb + jax.nn.log_sigmoid(-zf))
        hgrn.append((q_b, k_b, v_b, log_f))
    return gdn, hgrn, (g_a, g_b, m_a, m_b)


def gated_head_norm(o, g, gate, dtype):
    o = o * lax.rsqrt(jnp.mean(o * o, axis=-1, keepdims=True) + EPS) * g.astype(jnp.float32)
    return (from_heads(o) * jax.nn.silu(gate.astype(jnp.float32))).astype(dtype)


def merge_branches(oa, ob, gates, gdn_g, hgrn_g, w_up_a, w_up_b, w_out, dtype):
    g_a, g_b, m_a, m_b = gates
    ya = gated_head_norm(oa, gdn_g, g_a, dtype) @ w_up_a
    yb = gated_head_norm(ob, hgrn_g, g_b, dtype) @ w_up_b
    return (jax.nn.sigmoid(m_a) * ya + jax.nn.sigmoid(m_b) * yb) @ w_out


def token_mixer(h_lat, h_ctx, w_in, conv_w, a_log, dt_bias, lb, gdn_g, hgrn_g, w_up_a, w_up_b, w_out,
                rows, need_ctx_out):
    gdn_l, hgrn_l, gates_l = mixer_inputs(h_lat, w_in, conv_w, a_log, dt_bias, lb, rows)
    gdn_c, hgrn_c, gates_c = mixer_inputs(h_ctx, w_in, conv_w, a_log, dt_bias, lb, None)
    bsz = h_lat.shape[0]
    s0_a = jnp.zeros((bsz, H_A, DK_A, DV_A), jnp.float32)
    s0_b = jnp.zeros((bsz, H_B, DK_B, DV_B), jnp.float32)
    oa_lat, oa_ctx = run_bidirectional(gdn_chunk_scan, gdn_c, gdn_l, s0_a, need_ctx_out)
    ob_lat, ob_ctx = run_bidirectional(hgrn2_chunk_scan, hgrn_c, hgrn_l, s0_b, need_ctx_out)
    y_lat = merge_branches(oa_lat, ob_lat, gates_l, gdn_g, hgrn_g, w_up_a, w_up_b, w_out, h_lat.dtype)
    y_ctx = None
    if need_ctx_out:
        y_ctx = merge_branches(oa_ctx, ob_ctx, gates_c, gdn_g, hgrn_g, w_up_a, w_up_b, w_out, h_ctx.dtype)
    return y_lat, y_ctx


def swiglu(h, w_in, w_out):
    gate, up = jnp.split(h @ w_in, 2, axis=-1)
    return (jax.nn.silu(gate) * up) @ w_out


def setup_inputs(seed: int = 0) -> dict:
    key = jax.random.key(seed)
    ks = jax.random.split(key, 24)
    f32 = jnp.float32

    def nrm(k, shape, scale):
        return scale * jax.random.normal(k, shape, f32)

    dt = jnp.exp(jax.random.uniform(ks[10], (DEPTH, 2, H_A), f32, math.log(1e-3), math.log(1e-1)))
    return {
        "x": nrm(ks[0], (BATCH, SEQ, D_MODEL), 1.0),
        "c": nrm(ks[1], (BATCH, D_MODEL), 1.0),
        "ctx": nrm(ks[2], (BATCH, CTX_LEN, D_MODEL), 1.0),
        "c_ctx": nrm(ks[3], (D_MODEL,), 1.0),
        "mod_w": nrm(ks[4], (DEPTH, D_MODEL, 6 * D_MODEL), 0.5 * D_MODEL ** -0.5),
        "mod_b": nrm(ks[5], (DEPTH, 6 * D_MODEL), 0.01),
        "norm_mix_g": 1.0 + nrm(ks[6], (DEPTH, D_MODEL), 0.02),
        "norm_ffn_g": 1.0 + nrm(ks[7], (DEPTH, D_MODEL), 0.02),
        "w_in": nrm(ks[8], (DEPTH, D_MODEL, D_IN), D_MODEL ** -0.5),
        "conv_w": nrm(ks[9], (DEPTH, CONV_K, QKV_A), CONV_K ** -0.5),
        "a_log": jnp.log(jax.random.uniform(ks[11], (DEPTH, 2, H_A), f32, 1.0, 16.0)),
        "dt_bias": dt + jnp.log(-jnp.expm1(-dt)),
        "gdn_norm_g": 1.0 + nrm(ks[12], (DEPTH, DV_A), 0.02),
        "lb_logits": nrm(ks[13], (DEPTH + 1, H_B * DK_B), 0.1),
        "hgrn_norm_g": 1.0 + nrm(ks[14], (DEPTH, DV_B), 0.02),
        "w_up_a": nrm(ks[15], (DEPTH, W_A, D_MODEL), W_A ** -0.5),
        "w_up_b": nrm(ks[16], (DEPTH, W_B, D_MODEL), W_B ** -0.5),
        "w_out": nrm(ks[17], (DEPTH, D_MODEL, D_MODEL), D_MODEL ** -0.5),
        "ffn_w_in": nrm(ks[18], (DEPTH, D_MODEL, 2 * D_FF), D_MODEL ** -0.5),
        "ffn_w_out": nrm(ks[19], (DEPTH, D_FF, D_MODEL), D_FF ** -0.5),
        "final_norm_g": 1.0 + nrm(ks[20], (D_MODEL,), 0.02),
    }


def reference(x, c, ctx, c_ctx, mod_w, mod_b, norm_mix_g, norm_ffn_g, w_in, conv_w, a_log, dt_bias,
              gdn_norm_g, lb_logits, hgrn_norm_g, w_up_a, w_up_b, w_out, ffn_w_in, ffn_w_out, final_norm_g):
    rows = x.shape[1] // GRID_W
    lbs = jnp.cumsum(jax.nn.softmax(lb_logits.astype(jnp.float32), axis=0), axis=0)
    for layer in range(DEPTH):
        last = layer == DEPTH - 1
        m_lat = (jax.nn.silu(c) @ mod_w[layer] + mod_b[layer])[:, None, :]
        m_ctx = jax.nn.silu(c_ctx) @ mod_w[layer] + mod_b[layer]
        sh1, sc1, g1, sh2, sc2, g2 = jnp.split(m_lat, 6, axis=-1)
        csh1, csc1, cg1, csh2, csc2, cg2 = jnp.split(m_ctx, 6, axis=-1)
        h_lat = rms_norm(x, norm_mix_g[layer]) * (1.0 + sc1) + sh1
        h_ctx = rms_norm(ctx, norm_mix_g[layer]) * (1.0 + csc1) + csh1
        mix_lat, mix_ctx = token_mixer(h_lat, h_ctx, w_in[layer], conv_w[layer], a_log[layer], dt_bias[layer],
                                       lbs[layer], gdn_norm_g[layer], hgrn_norm_g[layer], w_up_a[layer],
                                       w_up_b[layer], w_out[layer], rows, not last)
        x = x + g1 * mix_lat
        x = x + g2 * swiglu(rms_norm(x, norm_ffn_g[layer]) * (1.0 + sc2) + sh2, ffn_w_in[layer], ffn_w_out[layer])
        if not last:
            ctx = ctx + cg1 * mix_ctx
            ctx = ctx + cg2 * swiglu(rms_norm(ctx, norm_ffn_g[layer]) * (1.0 + csc2) + csh2,
                                     ffn_w_in[layer], ffn_w_out[layer])
    return rms_norm(x, final_norm_g)
```

```python
import numpy as np
from contextlib import ExitStack
import concourse.bass as bass
import concourse.mybir as mybir
from concourse.bass_utils import run_bass_kernel_spmd

F32 = mybir.dt.float32
BF16 = mybir.dt.bfloat16
AF = mybir.ActivationFunctionType
ALU = mybir.AluOpType


class Res:
    __slots__ = ("name", "w", "r")

    def __init__(self, name=""):
        self.name = name
        self.w = None
        self.r = {}


class Sched:
    ENG = ("pe", "act", "dve", "pool", "sp")

    def __init__(self, nc, n_dma_sems=33):
        self.nc = nc
        self.es = ExitStack()
        self.prog = {e: [] for e in self.ENG}
        self.cnt = {e: 0 for e in self.ENG}
        self.sem = {e: self.es.enter_context(nc.semaphore("s_" + e)) for e in self.ENG if e != "sp"}
        self.dsem = [self.es.enter_context(nc.semaphore("d%d" % i)) for i in range(n_dma_sems)]
        self.dcnt = [0] * n_dma_sems
        self.dnext = 0
        self.dnext_sw = 0
        self.known = {e: {} for e in self.ENG}
        self.final_tokens = []
        self.npsum = 0
        self.ph = None
        self.nalloc = 0

    def begin_phase(self):
        self.ph = ExitStack()

    def end_phase(self, last=False):
        self.barrier()
        if last:
            self._waits("sp", self.final_tokens)
        self.emit_block()
        self.ph.close()
        self.ph = None
        if last:
            self.es.close()

    def barrier(self):
        toks = [(e, self.cnt[e]) for e in self.ENG if e != "sp" and self.cnt[e] > 0]
        toks += [(("d", i), v) for i, v in enumerate(self.dcnt) if v > 0]
        for e in self.ENG:
            self._waits(e, toks)

    def sbuf(self, name, shape, dtype, persist=False):
        self.nalloc += 1
        st = self.es if (persist or self.ph is None) else self.ph
        return st.enter_context(self.nc.sbuf_tensor("%s_%d" % (name, self.nalloc), list(shape), dtype))

    def buf(self, name, shape, dtype, persist=False):
        return (self.sbuf(name, shape, dtype, persist), Res(name))

    def ring(self, name, shape, dtype, n, persist=False):
        return Ring([self.buf("%s%d" % (name, i), shape, dtype, persist) for i in range(n)])

    def mm(self, out, lhsT, rhs, reads, writes, start=True, stop=True):
        return self.op("pe", lambda e: e.matmul(out, lhsT=lhsT, rhs=rhs, start=start, stop=stop), reads, writes)

    def tr(self, out, in_, ident, reads, writes):
        return self.op("pe", lambda e: e.transpose(out, in_, ident), reads, writes)

    def act(self, out, in_, func, reads, writes, scale=None, bias=None, accum_out=None):
        kw = {}
        if scale is not None:
            kw["scale"] = scale
        if bias is not None:
            kw["bias"] = bias
        if accum_out is not None:
            kw["accum_out"] = accum_out
        return self.op("act", lambda e: e.activation(out=out, in_=in_, func=func, **kw), reads, writes)

    def tt(self, eng, out, in0, in1, op, reads, writes):
        return self.op(eng, lambda e: e.tensor_tensor(out=out, in0=in0, in1=in1, op=op), reads, writes)

    def ts(self, eng, out, in0, s1, s2, op0, op1, reads, writes):
        if op1 is None:
            return self.op(eng, lambda e: e.tensor_scalar(out=out, in0=in0, scalar1=s1, scalar2=None, op0=op0), reads, writes)
        return self.op(eng, lambda e: e.tensor_scalar(out=out, in0=in0, scalar1=s1, scalar2=s2, op0=op0, op1=op1), reads, writes)

    def stt(self, out, in0, scalar, in1, op0, op1, reads, writes):
        return self.op("dve", lambda e: e.scalar_tensor_tensor(out=out, in0=in0, scalar=scalar, in1=in1, op0=op0, op1=op1), reads, writes)

    def cp(self, eng, out, in_, reads, writes):
        if eng == "act":
            return self.op("act", lambda e: e.copy(out=out, in_=in_), reads, writes)
        return self.op(eng, lambda e: e.tensor_copy(out=out, in_=in_), reads, writes)

    def psum(self, name, shape=(128, 512), dtype=F32):
        return self.es.enter_context(self.nc.psum_tensor(name, list(shape), dtype))

    def _semobj(self, key):
        if isinstance(key, tuple):
            return self.dsem[key[1]]
        return self.sem[key]

    def _waits(self, eng, toks):
        best = {}
        for (k, v) in toks:
            if best.get(k, 0) < v:
                best[k] = v
        kn = self.known[eng]
        for k, v in best.items():
            if kn.get(k, 0) >= v:
                continue
            kn[k] = v
            self.prog[eng].append(("wait", k, v))

    def _deps(self, eng, reads, writes):
        toks = []
        for r in reads:
            if r.w is not None:
                if not (r.w[0] == eng and eng == "pe"):
                    toks.append(r.w)
        for r in writes:
            if r.w is not None and not (r.w[0] == eng and eng == "pe"):
                toks.append(r.w)
            for k, t in r.r.items():
                if not (t[0] == eng and eng == "pe"):
                    toks.append(t)
        return toks

    def op(self, eng, fn, reads=(), writes=()):
        self._waits(eng, self._deps(eng, reads, writes))
        idx = self.cnt[eng]
        self.cnt[eng] += 1
        tok = (eng, idx + 1)
        self.prog[eng].append(("op", fn, idx))
        for r in reads:
            r.r[eng] = tok
        for r in writes:
            r.w = tok
            r.r = {}
        return tok

    def dma(self, out, in_, reads=(), writes=(), queue="sp", final=False, fn=None, inc=16):
        n_sw = 8
        if inc != 16:
            i = len(self.dsem) - 1
        elif queue == "pool":
            i = len(self.dsem) - 1 - n_sw + self.dnext_sw
            self.dnext_sw = (self.dnext_sw + 1) % n_sw
        else:
            i = self.dnext
            self.dnext = (self.dnext + 1) % (len(self.dsem) - 1 - n_sw)
        toks = self._deps(("d", i), reads, writes)
        if self.dcnt[i] > 0:
            toks.append((("d", i), self.dcnt[i]))
        self._waits(queue, toks)
        self.dcnt[i] += inc
        tok = (("d", i), self.dcnt[i])
        if fn is not None:
            self.prog[queue].append(("dmafn", fn, inc, i))
        else:
            self.prog[queue].append(("dma", out, in_, i))
        for r in reads:
            r.r[("d", i)] = tok
        for r in writes:
            r.w = tok
            r.r = {}
        if final:
            self.final_tokens.append(tok)
        return tok

    def emit(self):
        self._waits("sp", self.final_tokens)
        self.emit_block()
        self.es.close()

    def emit_block(self):
        nc = self.nc
        engmap = {"pe": "tensor", "act": "scalar", "dve": "vector", "pool": "gpsimd", "sp": "sync"}
        marks = {e: set() for e in self.sem}
        for ename in self.ENG:
            for item in self.prog[ename]:
                if item[0] == "wait" and not isinstance(item[1], tuple):
                    marks[item[1]].add(item[2] - 1)
        if not hasattr(self, "base"):
            self.base = {e: 0 for e in self.sem}
        val = {}
        for e in self.sem:
            c = self.base[e]
            val[e] = {}
            for item in self.prog[e]:
                if item[0] == "op" and item[2] in marks[e]:
                    c += 1
                    val[e][item[2]] = c
            missing = [m for m in marks[e] if m not in val[e]]
            assert not missing, (e, missing[:5])
            self.base[e] = c

        def run(ename, eng):
            for item in self.prog[ename]:
                if item[0] == "wait":
                    if isinstance(item[1], tuple):
                        eng.wait_ge(self._semobj(item[1]), item[2])
                    else:
                        eng.wait_ge(self.sem[item[1]], val[item[1]][item[2] - 1])
                elif item[0] == "op":
                    ins = item[1](eng)
                    if item[2] in marks[ename]:
                        ins.then_inc(self.sem[ename], 1)
                elif item[0] == "dmafn":
                    item[1](eng).then_inc(self.dsem[item[3]], item[2])
                else:
                    eng.dma_start(out=item[1], in_=item[2]).then_inc(self.dsem[item[3]], 16)

        with nc.Block() as block:
            for ename in self.ENG:
                getattr(block, engmap[ename])(lambda e, _n=ename: run(_n, e))
        for ename in self.ENG:
            self.prog[ename] = []


class Ring:
    def __init__(self, bufs):
        self.bufs = bufs
        self.i = 0

    def next(self):
        b = self.bufs[self.i % len(self.bufs)]
        self.i += 1
        return b


D = 1024
NTOK = 4096
NCTX = 256
NT = NTOK // 128
DIN = 6672
DFF = 2816
EPS = 1e-6
C_QKV, C_SM, C_QB, C_IB, C_F0, C_F1, C_GA, C_GB, C_MA, C_MB = 0, 1536, 1552, 2064, 2576, 3088, 3600, 4112, 4624, 5648
NCB = 26


def make_consts():
    c = np.zeros((128, NCB, 128), np.float32)
    j = np.arange(128)[:, None]
    i = np.arange(128)[None, :]
    c[:, 0] = (j == i)
    c[:, 1] = 1.0
    c[:, 2] = (j <= i)
    c[:, 3] = (j >= i)
    c[:, 4] = np.where(i > j, 0.0, -1e30)
    c[:, 5] = np.where(i < j, 0.0, -1e30)
    c[:, 10] = (j <= i)
    c[:, 11] = (j >= i)
    for l in range(7):
        b = 1 << l
        same = (j // (2 * b)) == (i // (2 * b))
        m0 = (j == i) | (same & ((j % (2 * b)) >= b) & ((i % (2 * b)) < b))
        c[:, 12 + l] = m0
        c[:, 19 + l] = m0.T
    r = np.ones((128, 512), np.float32)
    r[:, ::128] = 0.0
    c[:, 6:10] = r.reshape(128, 4, 128)
    return c.reshape(128, NCB * 128)


def build_program(stop_after=None):
    nc = bass.Bass("TRN2", target_bir_lowering=False)

    def din(name, shape):
        return nc.dram_tensor(name, list(shape), F32, kind="ExternalInput").ap()

    x_d = din("x", [NTOK, D])
    ctx_d = din("ctx", [NCTX, D])
    cvec_d = din("cvec", [2, D])
    modw_d = din("mod_w", [D, 6 * D])
    modb_d = din("mod_b", [1, 6 * D])
    nmg_d = din("norm_mix_g", [D])
    nfg_d = din("norm_ffn_g", [D])
    win_d = din("w_in", [D, DIN])
    convw_d = din("conv_w", [5, 1536])
    alog_d = din("a_log", [1, 8])
    dtb_d = din("dt_bias", [1, 8])
    gng_d = din("gdn_norm_g", [1, 128])
    lbl_d = din("lb_logits", [2, 512])
    hng_d = din("hgrn_norm_g", [1, 128])
    wua_d = din("w_up_a", [512, D])
    wub_d = din("w_up_b", [512, D])
    wout_d = din("w_out", [D, D])
    fwi_d = din("ffn_w_in", [D, 2 * DFF])
    fwo_d = din("ffn_w_out", [DFF, D])
    fng_d = din("final_norm_g", [1, D])
    cst_d = din("consts", [128, NCB * 128])
    sel_d = din("sel", [128, 2])
    out_d = nc.dram_tensor("out", [NTOK, D], F32, kind="ExternalOutput").ap()
    st_fm = nc.dram_tensor("st_fm", [NT, 128, 12, 128], BF16).ap()
    st_tm = nc.dram_tensor("st_tm", [NT, 128, 3, 512], BF16).ap()
    st_o = nc.dram_tensor("st_o", [NT, 128, 1024], F32).ap()
    st_x1 = nc.dram_tensor("st_x1", [NT, 128, 1024], F32).ap()
    st_gbc = nc.dram_tensor("st_gbc", [128, 2, 1024], F32).ap()
    ex_in = nc.dram_tensor("ex_in", [128, 1024], F32).ap()
    ex_out = nc.dram_tensor("ex_out", [256, 1024], F32).ap()

    S = Sched(nc)
    MUL, ADD, SUB, POW = ALU.mult, ALU.add, ALU.subtract, ALU.pow

    cst, r_cst = S.buf("cst", [128, 6, 128], F32, True)
    cstb, r_cstb = S.buf("cstb", [128, NCB, 128], BF16, True)
    S.dma(cst[:], cst_d.rearrange("p (b n) -> p b n", n=128)[:, 0:6, :], writes=[r_cst])
    S.dma(cstb[:], cst_d.rearrange("p (b n) -> p b n", n=128), writes=[r_cstb], queue="pool")
    ident_f, ones_f = cst[:, 0, :], cst[:, 1, :]
    ident_b, ones_b = cstb[:, 0, :], cstb[:, 1, :]
    RC = [r_cst, r_cstb]
    Sg, r_Sg = S.buf("Sg", [128, 4, 128], F32, True)
    Sh, r_Sh = S.buf("Sh", [128, 4, 128], F32, True)
    Sgb, r_Sgb = S.buf("Sgb", [128, 4, 128], BF16, True)
    Shb, r_Shb = S.buf("Shb", [128, 4, 128], BF16, True)
    for t_, r_ in ((Sg, r_Sg), (Sh, r_Sh), (Sgb, r_Sgb), (Shb, r_Shb)):
        S.op("pool", lambda e, t_=t_: e.memset(t_[:], 0.0), [], [r_])
    modT, r_modT = S.buf("modT", [128, 8, 8], F32, True)
    sm_c, r_smc = S.buf("sm_c", [128, 64], F32, True)
    sel, r_sel = S.buf("sel", [128, 2], F32, True)
    S.dma(sel[:], sel_d, writes=[r_sel])
    gvec, r_gvec = S.buf("gvec", [128, 2, 128], F32, True)
    S.dma(gvec[:, 0, :], gng_d.partition_broadcast(128), writes=[r_gvec])
    S.dma(gvec[:, 1, :], hng_d.partition_broadcast(128), writes=[r_gvec])

    psq = []
    for i in range(5):
        bk = S.psum("psq%d" % i)
        psq.append((bk[:, 0:128], Res("psq%d" % i)))
        pswl = locals().setdefault("pswl", [])
        pswl.append((bk, psq[-1][1]))
    psw = Ring(pswl)
    psq = Ring(psq)
    psf = Ring([(S.psum("psf%d" % i), Res("psf%d" % i)) for i in range(2)])
    bkb = S.psum("psb", (128, 1024), BF16)
    r_bkb = Res("psb")
    psb = Ring([(bkb[:, q * 256:q * 256 + 128], r_bkb) for q in range(4)])

    S.begin_phase()
    S.op("pool", lambda e: e.memset(sm_c[:, 0:1], -0.5), [], [r_smc])
    S.op("pool", lambda e: e.memset(sm_c[:, 32:33], EPS), [], [r_smc])
    cv, r_cv = S.buf("cv", [128, 2, 8], F32)
    S.dma(None, None, writes=[r_cv],
          fn=lambda e: e.dma_start(out=cv[:], in_=cvec_d.rearrange("v (k p) -> p v k", p=128), allow_slow_non_contiguous=True))
    scv, r_scv = S.buf("scv", [128, 2, 8], F32)
    S.act(scv[:], cv[:], AF.Silu, [r_cv], [r_scv])
    crep, r_crep = S.buf("crep", [128, 2, 8, 128], F32)
    for v in range(2):
        for k in range(8):
            S.ts("pool", crep[:, v, k, :], ones_f, scv[:, v, k:k + 1], None, MUL, None, [r_scv, r_cst], [r_crep])
    mb, r_mb = S.buf("mb", [1, 6 * D], F32)
    S.dma(mb[:], modb_d, writes=[r_mb])
    mwr = S.ring("mw", [128, 8, 512], F32, 2)
    bc, r_bc = S.buf("bc", [128, 6 * D], F32)
    bcc, r_bcc = S.buf("bcc", [128, 2 * D], F32)
    for cb in range(12):
        mw, r_mw = mwr.next()
        S.dma(mw[:], modw_d[:, cb * 512:(cb + 1) * 512].rearrange("(k p) n -> p k n", p=128), writes=[r_mw])
        for v in range(2):
            if v == 1 and cb >= 4:
                continue
            pb, r_pb = psf.next()
            for k in range(8):
                S.mm(pb[:, :], crep[:, v, k, :], mw[:, k, :], [r_crep, r_mw], [r_pb], start=(k == 0), stop=False)
            S.mm(pb[:, :], ones_f[0:1, :], mb[0:1, cb * 512:(cb + 1) * 512], [r_cst, r_mb], [r_pb], start=False, stop=True)
            if v == 0:
                S.cp("act", bc[:, cb * 512:(cb + 1) * 512], pb[:, :], [r_pb], [r_bc])
            else:
                S.cp("dve", bcc[:, cb * 512:(cb + 1) * 512], pb[:, :], [r_pb], [r_bcc])
    S.dma(st_gbc[:, 0, :], bc[:, 2 * D:3 * D], reads=[r_bc])
    S.dma(st_gbc[:, 1, :], bc[:, 5 * D:6 * D], reads=[r_bc])
    mT, r_mT = S.buf("mT", [128, 6, 8], F32)
    pq, r_pq = psq.next()
    for w_, (src, blk) in enumerate(((bc, 0), (bc, 1), (bc, 3), (bc, 4), (bcc, 0), (bcc, 1))):
        for k in range(8):
            S.mm(pq[:, w_ * 8 + k:w_ * 8 + k + 1], src[0:1, blk * D + k * 128: blk * D + (k + 1) * 128], ones_f[0:1, 0:1],
                 [r_bc, r_bcc, r_cst], [r_pq])
    S.cp("dve", mT[:].rearrange("p a b -> p (a b)"), pq[:, 0:48], [r_pq], [r_mT])
    ng, r_ng = S.buf("ng", [128, 2, 8], F32)
    S.dma(None, None, writes=[r_ng], fn=lambda e: e.dma_start(out=ng[:, 0, :], in_=nmg_d.rearrange("(k p) -> p k", p=128), allow_slow_non_contiguous=True))
    S.dma(None, None, writes=[r_ng], fn=lambda e: e.dma_start(out=ng[:, 1, :], in_=nfg_d.rearrange("(k p) -> p k", p=128), allow_slow_non_contiguous=True))
    for dst, sc_i, sh_i, g_i in ((0, 1, 0, 0), (2, 3, 2, 1), (4, 5, 4, 0)):
        S.stt(modT[:, dst, :], mT[:, sc_i, :], 1.0, ng[:, g_i, :], ADD, MUL, [r_mT, r_ng], [r_modT])
        S.cp("dve", modT[:, dst + 1, :], mT[:, sh_i, :], [r_mT], [r_modT])
    lbl, r_lbl = S.buf("lbl", [128, 2, 4], F32)
    S.dma(None, None, writes=[r_lbl], fn=lambda e: e.dma_start(out=lbl[:], in_=lbl_d.rearrange("r (h p) -> p r h", p=128), allow_slow_non_contiguous=True))
    t4, r_t4 = S.buf("t4", [128, 4], F32)
    S.tt("dve", t4[:], lbl[:, 1, :], lbl[:, 0, :], SUB, [r_lbl], [r_t4])
    S.act(t4[:], t4[:], AF.Exp, [r_t4], [r_t4])
    S.ts("dve", t4[:], t4[:], 1.0, None, ADD, None, [r_t4], [r_t4])
    S.op("dve", lambda e: e.reciprocal(out=sm_c[:, 1:5], in_=t4[:]), [r_t4], [r_smc])
    S.ts("dve", sm_c[:, 5:9], sm_c[:, 1:5], -0.5, 0.5, MUL, ADD, [r_smc], [r_smc])
    S.ts("dve", sm_c[:, 9:13], sm_c[:, 1:5], 0.5, 0.5, MUL, ADD, [r_smc], [r_smc])
    S.dma(sm_c[:, 16:24], dtb_d.partition_broadcast(128), writes=[r_smc])
    al, r_al = S.buf("al", [128, 8], F32)
    S.dma(al[:], alog_d.partition_broadcast(128), writes=[r_al])
    S.act(al[:], al[:], AF.Exp, [r_al], [r_al])
    S.ts("dve", sm_c[:, 24:32], al[:], -1.0, None, MUL, None, [r_al], [r_smc])
    S.end_phase(last=(stop_after == "0"))
    if stop_after == "0":
        return nc
    ctxv = dict(locals())
    scan_pass(ctxv, 0)
    if stop_after != "A":
        ctxv = dict(locals())
        exchange(ctxv)
        scan_pass(ctxv, 1)
        dense_c1(ctxv)
        dense_c2(ctxv)
        return nc
    if stop_after == "A":
        S.begin_phase()
        import os
        ntl = NT if not os.environ.get("KLIM") else 4 * (int(os.environ["KLIM"]) - 1)
        for tl in range(ntl):
            S.dma(out_d[tl * 128:(tl + 1) * 128, :], st_o[tl], final=True)
        S.end_phase(last=True)
        return nc
    return nc


def scan_pass(cx, d):
    S = cx["S"]; nc = cx["nc"]
    MUL, ADD, SUB, POW = ALU.mult, ALU.add, ALU.subtract, ALU.pow
    cst, cstb, r_cst, r_cstb = cx["cst"], cx["cstb"], cx["r_cst"], cx["r_cstb"]
    ident_f, ones_f, ident_b, ones_b = cx["ident_f"], cx["ones_f"], cx["ident_b"], cx["ones_b"]
    psq, psf, psb, psw, bkb, r_bkb = cx["psq"], cx["psf"], cx["psb"], cx["psw"], cx["bkb"], cx["r_bkb"]
    Sg, Sh, Sgb, Shb = cx["Sg"], cx["Sh"], cx["Sgb"], cx["Shb"]
    r_Sg, r_Sh, r_Sgb, r_Shb = cx["r_Sg"], cx["r_Sh"], cx["r_Sgb"], cx["r_Shb"]
    modT, r_modT, sm_c, r_smc = cx["modT"], cx["r_modT"], cx["sm_c"], cx["r_smc"]
    x_d, ctx_d, win_d, convw_d = cx["x_d"], cx["ctx_d"], cx["win_d"], cx["convw_d"]
    st_fm, st_tm, st_o = cx["st_fm"], cx["st_tm"], cx["st_o"]
    tri_f = cst[:, 2 + d, :]
    MS = cst[:, 4 + d, :]
    MI = cstb[:, 10 + d, :]
    ML = [cstb[:, 12 + 7 * d + l, :] for l in range(7)]
    rst = cstb[:, 6:10, :]
    S.begin_phase()
    if d == 0:
        NW = 3088
        o_qkv, o_sm, o_qb, o_ib, o_f = 0, 1536, 1552, 2064, 2576
        wA, r_wA = S.buf("wA", [128, 8, NW], BF16)
        for k in range(8):
            S.dma(wA[:, k, :], win_d[k * 128:(k + 1) * 128, 0:NW], writes=[r_wA], queue="pool")
    else:
        NW = 528
        o_sm, o_f = 0, 16
        wA, r_wA = S.buf("wA", [128, 8, NW], BF16)
        for k in range(8):
            S.dma(wA[:, k, 0:16], win_d[k * 128:(k + 1) * 128, C_SM:C_SM + 16], writes=[r_wA], queue="pool")
            S.dma(wA[:, k, 16:528], win_d[k * 128:(k + 1) * 128, C_F1:C_F1 + 512], writes=[r_wA], queue="pool")
    xring = S.ring("xt", [128, D], F32, 1)
    xnring = S.ring("xn", [128, D], F32, 1)
    st1 = S.ring("st1", [128, 4], F32, 4)
    hT, r_hT = S.buf("hT", [128, 8, 512], BF16)
    qkT, r_qkT = S.buf("qkT", [128, 8, 512], BF16)
    ktok, r_ktok = S.buf("ktok", [128, 4, 512], BF16)
    vtok, r_vtok = S.buf("vtok", [128, 4, 512], BF16)
    qBT, r_qBT = S.buf("qBT", [128, 4, 512], BF16)
    vBtok, r_vBtok = S.buf("vBtok", [128, 4, 512], BF16)
    fT, r_fT = S.buf("fT", [128, 4, 512], F32)
    lg, r_lg = S.buf("lg", [128, 4, 512], F32)
    kTh, r_kTh = S.buf("kTh", [128, 4, 512], BF16)
    qtT, r_qtT = S.buf("qtT", [128, 4, 512], BF16)
    ktT, r_ktT = S.buf("ktT", [128, 4, 512], BF16)
    smraw, r_smraw = S.buf("smraw", [128, 4, 16], F32)
    f512 = S.ring("f512", [128, 512], F32, 3)
    b512 = S.ring("b512", [128, 512], BF16, 2)
    svr = S.ring("svA", [128, 4, 8], F32, 6)
    H4 = [128, 4, 128]
    G = {
        "lar": S.buf("lar4", H4, F32), "dti": S.buf("dti4", H4, F32), "tmp": S.buf("tmp4", H4, F32), "ke": S.buf("ke4", H4, F32),
        "negU": S.buf("negU4", H4, BF16), "xy": S.ring("xy4", H4, BF16, 6),
        "Y": [S.buf("Y7_%d" % p_, H4, BF16) for p_ in range(2)],
        "A": [S.buf("A4_%d" % p_, H4, BF16) for p_ in range(2)],
        "K": [S.buf("K4_%d" % p_, H4, BF16) for p_ in range(2)],
        "ge": [S.buf("ge4_%d" % p_, [128, 8], F32) for p_ in range(2)],
        "rp": S.buf("rp4", H4, BF16), "uc": S.buf("uc4", H4, BF16),
        "keT": S.buf("keT4", H4, BF16), "kendh": S.buf("kendh4", H4, BF16), "atb": S.buf("atb4", H4, BF16),
    }
    s8 = S.ring("s8", [128, 8], F32, 8)
    oring = S.ring("ot", [128, D], F32, 2)
    if d == 1:
        od0r = S.ring("od0", [128, D], F32, 2)
    if d == 0:
        ypad, r_ypad = S.buf("ypad", [128, 12, 544], BF16)
        S.op("pool", lambda e: e.memset(ypad[:], 0.0), [], [r_ypad])
        cwT, r_cwT = S.buf("cwT", [128, 5, 12], F32)
        for j in range(5):
            S.dma(None, None, writes=[r_cwT], fn=lambda e, j=j: e.dma_start(
                out=cwT[:, j, :], in_=convw_d[j].rearrange("(c p) -> p c", p=128), allow_slow_non_contiguous=True))
        dg, r_dg = S.buf("dg", [128, 12, 5, 128], BF16)
        for c in range(12):
            for j in range(5):
                S.ts("pool", dg[:, c, j, :], ident_f, cwT[:, j, c:c + 1], None, MUL, None, [r_cwT, r_cst], [r_dg])

    pall = Ring(list(psw.bufs) + list(psf.bufs))
    if d == 0:
        groups = [("ctx", 0, 2)] + [("lat", g * 4, 4) for g in range(8)]
    else:
        groups = [("lat", g * 4, 4) for g in range(7, -1, -1)]
    import os
    if os.environ.get("KLIM"):
        kl = int(os.environ["KLIM"])
        groups = groups[:kl] if d == 0 else [("lat", g * 4, 4) for g in range(kl - 2, -1, -1)]

    for (kind, tile0, T) in groups:
        ntok = T * 128
        lat = kind == "lat"
        src = x_d if lat else ctx_d
        Ai, Bi = (0, 1) if lat else (4, 5)
        for t in range(T):
            xt, r_xt = xring.next()
            xn, r_xn = xnring.next()
            S.dma(xt[:], src[(tile0 + t) * 128:(tile0 + t + 1) * 128, :], writes=[r_xt])
            ss, r_ss = st1.next()
            S.act(xn[:], xt[:], AF.Square, [r_xt], [r_xn, r_ss], accum_out=ss[:, 0:1])
            S.ts("dve", ss[:, 1:2], ss[:, 0:1], 1.0 / D, EPS, MUL, ADD, [r_ss], [r_ss])
            S.tt("pool", ss[:, 2:3], ss[:, 1:2], sm_c[:, 0:1], POW, [r_ss, r_smc], [r_ss])
            S.ts("dve", xn[:], xt[:], ss[:, 2:3], None, MUL, None, [r_xt, r_ss], [r_xn])
            for half in range(2):
                pb, r_pb = pall.next()
                for q in range(4):
                    k = half * 4 + q
                    S.tr(pb[:, q * 128:(q + 1) * 128], xn[:, k * 128:(k + 1) * 128], ident_f, [r_xn, r_cst], [r_pb])
                for q in range(4):
                    k = half * 4 + q
                    o_ap = hT[:, k, t * 128:(t + 1) * 128]
                    i_ap = pb[:, q * 128:(q + 1) * 128]
                    if q % 2 == 0:
                        S.act(o_ap, i_ap, AF.Identity, [r_pb, r_modT], [r_hT], scale=modT[:, Ai, k:k + 1], bias=modT[:, Bi, k:k + 1])
                    else:
                        S.ts("dve", o_ap, i_ap, modT[:, Ai, k:k + 1], modT[:, Bi, k:k + 1], MUL, ADD, [r_pb, r_modT], [r_hT])

        def proj_fm(col0):
            pb, r_pb = pall.next()
            for k in range(8):
                S.mm(pb[:, 0:ntok], wA[:, k, col0:col0 + 128], hT[:, k, 0:ntok], [r_wA, r_hT], [r_pb], start=(k == 0), stop=(k == 7))
            return pb, r_pb

        for t in range(T):
            pq, r_pq = psq.next()
            for k in range(8):
                S.mm(pq[:, 0:16], hT[:, k, t * 128:(t + 1) * 128], wA[:, k, o_sm:o_sm + 16], [r_wA, r_hT], [r_pq], start=(k == 0), stop=(k == 7))
            S.cp("dve", smraw[:, t, :], pq[:, 0:16], [r_pq], [r_smraw])
            if d == 0:
                pb, r_pb = pall.next()
                for k in range(8):
                    S.mm(pb[:, :], hT[:, k, t * 128:(t + 1) * 128], wA[:, k, o_ib:o_ib + 512], [r_wA, r_hT], [r_pb], start=(k == 0), stop=(k == 7))
                S.cp("act", vBtok[:, t, :], pb[:, :], [r_pb], [r_vBtok])
        if d == 0:
            if lat:
                ypv = ypad[:].rearrange("p c (r w) -> p c r w", w=68)
            for c in range(12):
                pb, r_pb = proj_fm(o_qkv + c * 128)
                if lat:
                    S.cp("act" if c % 2 else "dve", ypv[:, c, 0:2 * T, 2:66], pb[:, 0:ntok].rearrange("p (r w) -> p r w", w=64), [r_pb], [r_ypad])
                else:
                    S.cp("act" if c % 2 else "dve", ypad[:, c, 2:2 + ntok], pb[:, 0:ntok], [r_pb], [r_ypad])
            for c in range(12):
                pb, r_pb = pall.next()
                for j in range(5):
                    rhs = ypv[:, c, 0:2 * T, j:j + 64] if lat else ypad[:, c, j:j + ntok]
                    S.mm(pb[:, 0:ntok], dg[:, c, j, :], rhs, [r_dg, r_ypad], [r_pb], start=(j == 0), stop=(j == 4))
                if c < 8:
                    qf, r_qf = f512.next()
                    S.act(qf[:, 0:ntok], pb[:, 0:ntok], AF.Silu, [r_pb], [r_qf])
                    sq, r_sq = b512.next()
                    S.tt("pool", sq[:, 0:ntok], qf[:, 0:ntok], qf[:, 0:ntok], MUL, [r_qf], [r_sq])
                    p2, r_p2 = pall.next()
                    S.mm(p2[:, 0:ntok], ones_b, sq[:, 0:ntok], [r_cstb, r_sq], [r_p2])
                    rk, r_rk = f512.next()
                    S.act(rk[:, 0:ntok], p2[:, 0:ntok], AF.Ln, [r_p2], [r_rk], bias=sm_c[:, 32:33])
                    S.act(rk[:, 0:ntok], rk[:, 0:ntok], AF.Exp, [r_rk], [r_rk], scale=-0.5)
                    S.stt(qkT[:, c, 0:ntok], qf[:, 0:ntok], (128.0 ** -0.5) if c < 4 else 1.0, rk[:, 0:ntok], MUL, MUL, [r_qf, r_rk], [r_qkT])
                    if c >= 4:
                        for t in range(T):
                            S.tr(bkb[:, t * 128:(t + 1) * 128], qkT[:, c, t * 128:(t + 1) * 128], ident_b, [r_qkT, r_cstb], [r_bkb])
                        S.cp("act", ktok[:, 0:T, (c - 4) * 128:(c - 3) * 128], bkb[:, 0:ntok].rearrange("p (t n) -> p t n", n=128), [r_bkb], [r_ktok])
                else:
                    vT, r_vT = b512.next()
                    S.act(vT[:, 0:ntok], pb[:, 0:ntok], AF.Silu, [r_pb], [r_vT])
                    for t in range(T):
                        S.tr(bkb[:, t * 128:(t + 1) * 128], vT[:, t * 128:(t + 1) * 128], ident_b, [r_vT, r_cstb], [r_bkb])
                    S.cp("dve", vtok[:, 0:T, (c - 8) * 128:(c - 7) * 128], bkb[:, 0:ntok].rearrange("p (t n) -> p t n", n=128), [r_bkb], [r_vtok])
            for h in range(4):
                pb, r_pb = proj_fm(o_qb + h * 128)
                S.act(qBT[:, h, 0:ntok], pb[:, 0:ntok], AF.Silu, [r_pb], [r_qBT])
            if not lat:
                S.op("pool", lambda e: e.memset(ypad[:], 0.0), [], [r_ypad])
        else:
            for t in range(T):
                tl = tile0 + t
                S.dma(qkT[:, :, t * 128:(t + 1) * 128], st_fm[tl, :, 0:8, :], writes=[r_qkT])
                S.dma(qBT[:, :, t * 128:(t + 1) * 128], st_fm[tl, :, 8:12, :], writes=[r_qBT])
                S.dma(ktok[:, t, :], st_tm[tl, :, 0, :], writes=[r_ktok])
                S.dma(vtok[:, t, :], st_tm[tl, :, 1, :], writes=[r_vtok])
                S.dma(vBtok[:, t, :], st_tm[tl, :, 2, :], writes=[r_vBtok])
        for h in range(4):
            pb, r_pb = proj_fm(o_f + h * 128)
            th, r_th = f512.next()
            S.act(th[:, 0:ntok], pb[:, 0:ntok], AF.Tanh, [r_pb], [r_th], scale=0.5)
            S.ts("dve", fT[:, h, 0:ntok], th[:, 0:ntok], sm_c[:, 5 + h:6 + h], sm_c[:, 9 + h:10 + h], MUL, ADD, [r_th, r_smc], [r_fT])
            S.ts("pool", kTh[:, h, 0:ntok], fT[:, h, 0:ntok], -1.0, 1.0, MUL, ADD, [r_fT], [r_kTh])
        if d == 0 and lat:
            for t in range(T):
                tl = tile0 + t
                S.dma(st_fm[tl, :, 0:8, :], qkT[:, :, t * 128:(t + 1) * 128], reads=[r_qkT])
                S.dma(st_fm[tl, :, 8:12, :], qBT[:, :, t * 128:(t + 1) * 128], reads=[r_qBT])
                S.dma(st_tm[tl, :, 0, :], ktok[:, t, :], reads=[r_ktok])
                S.dma(st_tm[tl, :, 1, :], vtok[:, t, :], reads=[r_vtok])
                S.dma(st_tm[tl, :, 2, :], vBtok[:, t, :], reads=[r_vBtok])
        for h in range(4):
            S.act(fT[:, h, 0:ntok], fT[:, h, 0:ntok], AF.Ln, [r_fT], [r_fT])
            S.op("dve", lambda e, h=h, ntok=ntok, T=T: e.tensor_tensor_scan(out=lg[:, h, 0:ntok], data0=rst[:, 0:T, :].rearrange("p a b -> p (a b)"),
                                                            data1=fT[:, h, 0:ntok], initial=0.0, op0=MUL, op1=ADD), [r_fT, r_cstb], [r_lg])
            if d == 1:
                S.stt(lg[:, h, 0:ntok], lg[:, h, 0:ntok], -1.0, fT[:, h, 0:ntok], MUL, ADD, [r_lg, r_fT], [r_lg])
                tot, r_tot = s8.next()
                for t in range(T):
                    c1_ = t * 128 + 127
                    S.tt("dve", tot[:, t:t + 1], fT[:, h, c1_:c1_ + 1], lg[:, h, c1_:c1_ + 1], SUB, [r_fT, r_lg], [r_tot])
                for t in range(T):
                    S.ts("dve", lg[:, h, t * 128:(t + 1) * 128], lg[:, h, t * 128:(t + 1) * 128], tot[:, t:t + 1], None, ADD, None, [r_lg, r_tot], [r_lg])
            e1, r_e1 = f512.next()
            S.act(e1[:, 0:ntok], lg[:, h, 0:ntok], AF.Exp, [r_lg], [r_e1])
            S.tt("dve", qtT[:, h, 0:ntok], qBT[:, h, 0:ntok], e1[:, 0:ntok], MUL, [r_qBT, r_e1], [r_qtT])
            e2, r_e2 = f512.next()
            S.act(e2[:, 0:ntok], lg[:, h, 0:ntok], AF.Exp, [r_lg], [r_e2], scale=-1.0)
            S.tt("pool", ktT[:, h, 0:ntok], kTh[:, h, 0:ntok], e2[:, 0:ntok], MUL, [r_kTh, r_e2], [r_ktT])

        order = list(range(T)) if d == 0 else list(range(T - 1, -1, -1))
        with_out = lat
        svA, r_sv = svr.next()
        gvA, r_gv = svr.next()
        egA, r_eg = svr.next()
        TT = slice(0, T)
        def bt4(ap4):
            return ap4.unsqueeze(1).to_broadcast([128, T, 4])
        S.tt("dve", svA[:, TT, 0:4], smraw[:, TT, 4 * d:4 * d + 4], bt4(sm_c[:, 16 + 4 * d:20 + 4 * d]), ADD, [r_smraw, r_smc], [r_sv])
        S.act(svA[:, TT, 0:4], svA[:, TT, 0:4], AF.Exp, [r_sv], [r_sv])
        S.act(svA[:, TT, 0:4], svA[:, TT, 0:4], AF.Ln, [r_sv], [r_sv], bias=1.0)
        S.tt("dve", svA[:, TT, 0:4], svA[:, TT, 0:4], bt4(sm_c[:, 24 + 4 * d:28 + 4 * d]), MUL, [r_sv, r_smc], [r_sv])
        S.act(svA[:, TT, 4:8], smraw[:, TT, 8 + 4 * d:12 + 4 * d], AF.Exp, [r_smraw], [r_sv], scale=-1.0)
        S.ts("dve", svA[:, TT, 4:8], svA[:, TT, 4:8], 1.0, None, ADD, None, [r_sv], [r_sv])
        S.op("dve", lambda e, svA=svA, TT=TT: e.reciprocal(out=svA[:, TT, 4:8], in_=svA[:, TT, 4:8]), [r_sv], [r_sv])
        pq, r_pq = psq.next()
        for t in range(T):
            S.mm(pq[:, t * 4:(t + 1) * 4], tri_f, svA[:, t, 0:4], [r_cst, r_sv], [r_pq])
        pq3 = pq[:, 0:4 * T].rearrange("p (t n) -> p t n", n=4)
        S.ts("dve", gvA[:, TT, 0:4], pq3, -1.0, None, MUL, None, [r_pq], [r_gv])
        S.act(egA[:, TT, 0:4], pq3, AF.Exp, [r_pq], [r_eg])
        S.ts("dve", gvA[:, TT, 4:8], egA[:, TT, 0:4], -1.0, None, MUL, None, [r_eg], [r_gv])
        S.ts("dve", egA[:, TT, 4:8], svA[:, TT, 4:8], -1.0, None, MUL, None, [r_sv], [r_eg])

        def bh(ap2):
            return ap2.unsqueeze(1).to_broadcast([128, 4, 128])

        def bl(ap4):
            return ap4.unsqueeze(2).to_broadcast([128, 4, 128])

        def v3(ap):
            return ap.rearrange("p (h n) -> p h n", h=4)

        def gdn_stage1(t, par):
            tsl = slice(t * 128, (t + 1) * 128)
            sv, gv, eg = svA[:, t, :], gvA[:, t, :], egA[:, t, :]
            ge4, r_ge = G["ge"][par]
            lar4, r_lar = G["lar"]
            dti4, r_dti = G["dti"]
            negU4, r_negU = G["negU"]
            S.tt("dve", lar4[:], bl(sv[:, 0:4]), bh(ones_f), MUL, [r_sv, r_cst], [r_lar])
            pg, r_pg = psw.next()
            pg3 = v3(pg[:, :])
            for h in range(4):
                S.mm(pg3[:, h, :], lar4[:, h, :], tri_f, [r_lar, r_cst], [r_pg])
            cg = 127 if d == 0 else 0
            S.cp("act", ge4[:, 0:4].unsqueeze(2), pg3[:, :, cg:cg + 1], [r_pg], [r_ge])
            S.act(ge4[:, 4:8], ge4[:, 0:4], AF.Exp, [r_ge], [r_ge])
            S.tt("dve", lar4[:], pg3, bh(MS), ADD, [r_pg, r_cst], [r_lar])
            yield
            S.tt("dve", lar4[:], lar4[:], bl(gv[:, 0:4]), ADD, [r_lar, r_gv], [r_lar])
            S.act(lar4[:], lar4[:], AF.Exp, [r_lar], [r_lar])
            if with_out:
                S.tt("dve", dti4[:], lar4[:], bh(ident_f), ADD, [r_lar, r_cst], [r_dti])
            pk, r_pk = psw.next()
            pk3 = v3(pk[:, :])
            for h in range(4):
                S.mm(pk3[:, h, :], qkT[:, 4 + h, tsl], qkT[:, 4 + h, tsl], [r_qkT], [r_pk])
            S.tt("dve", lar4[:], pk3, lar4[:], MUL, [r_pk, r_lar, r_dti], [r_lar])
            S.tt("dve", negU4[:], lar4[:], bl(eg[:, 4:8]), MUL, [r_lar, r_eg], [r_negU])
            yield
            if with_out:
                pa, r_pa = psw.next()
                pa3 = v3(pa[:, :])
                for h in range(4):
                    S.mm(pa3[:, h, :], qkT[:, 4 + h, tsl], qkT[:, h, tsl], [r_qkT], [r_pa])
                attT4, r_attT = G["A"][par]
                S.tt("dve", attT4[:], pa3, dti4[:], MUL, [r_pa, r_dti], [r_attT])
                yield
            ks4, r_ks = s8.next()
            S.tt("dve", ks4[:, 0:4], gv[:, 0:4], ge4[:, 0:4], ADD, [r_gv, r_ge], [r_ks])
            S.act(ks4[:, 0:4], ks4[:, 0:4], AF.Exp, [r_ks], [r_ks])
            kend4, r_kend = G["K"][par]
            S.tt("dve", kend4[:], v3(ktok[:, t, :]), bl(ks4[:, 0:4]), MUL, [r_ktok, r_ks], [r_kend])
            Xc = Yc = None
            r_X = r_Y = r_cstb
            for l in range(7):
                pm, r_pm = psw.next()
                pm3 = v3(pm[:, :])
                for h in range(4):
                    S.mm(pm3[:, h, :], ident_b, ident_b, [r_cstb], [r_pm], start=True, stop=False)
                    S.mm(pm3[:, h, :], negU4[:, h, :], ident_b if Xc is None else Xc[:, h, :], [r_negU, r_X], [r_pm], start=False, stop=True)
                Mp, r_Mp = G["xy"].next()
                S.tt("dve", Mp[:], pm3, bh(ML[l]), MUL, [r_pm, r_cstb], [r_Mp])
                yield
                if l < 6:
                    if l == 0:
                        Xn, r_Xn = Mp, r_Mp
                    else:
                        px, r_px = psw.next()
                        px3 = v3(px[:, :])
                        for h in range(4):
                            S.mm(px3[:, h, :], Yc[:, h, :], Mp[:, h, :], [r_Y, r_Mp], [r_px])
                        Xn, r_Xn = G["xy"].next()
                        S.cp("act", Xn[:], px3, [r_px], [r_Xn])
                py, r_py = psw.next()
                py3 = v3(py[:, :])
                for h in range(4):
                    S.mm(py3[:, h, :], Mp[:, h, :], ident_b if Yc is None else Yc[:, h, :], [r_Mp, r_Y], [r_py])
                if l < 6:
                    Yn, r_Yn = G["xy"].next()
                else:
                    Yn, r_Yn = G["Y"][par]
                S.cp("act" if l % 2 else "dve", Yn[:], py3, [r_py], [r_Yn])
                yield
                if l < 6:
                    Xc, r_X = Xn, r_Xn
                Yc, r_Y = Yn, r_Yn

        def gdn_stage2(t, par, ot, r_ot, od0, r_od0):
            tsl = slice(t * 128, (t + 1) * 128)
            sv, gv, eg = svA[:, t, :], gvA[:, t, :], egA[:, t, :]
            ge4, r_ge = G["ge"][par]
            Y7, r_Y7 = G["Y"][par]
            kend4, r_kend = G["K"][par]
            tmp4, r_tmp = G["tmp"]
            rp4, r_rp = G["rp"]
            uc4, r_uc = G["uc"]
            p1, r_p1 = psw.next()
            p13 = v3(p1[:, :])
            for h in range(4):
                S.mm(p13[:, h, :], qkT[:, 4 + h, tsl], Sgb[:, h, :], [r_qkT, r_Sgb], [r_p1])
            S.tt("dve", tmp4[:], p13, bl(gv[:, 4:8]), MUL, [r_p1, r_gv], [r_tmp])
            S.tt("dve", rp4[:], tmp4[:], v3(vtok[:, t, :]), ADD, [r_tmp, r_vtok], [r_rp])
            yield
            p2, r_p2 = psw.next()
            p23 = v3(p2[:, :])
            for h in range(4):
                S.mm(p23[:, h, :], Y7[:, h, :], rp4[:, h, :], [r_Y7, r_rp], [r_p2])
            S.tt("dve", uc4[:], p23, bl(sv[:, 4:8]), MUL, [r_p2, r_sv], [r_uc])
            yield
            if with_out:
                attT4, r_attT = G["A"][par]
                p4, r_p4 = psw.next()
                p43 = v3(p4[:, :])
                for h in range(4):
                    S.mm(p43[:, h, :], qkT[:, h, tsl], Sgb[:, h, :], [r_qkT, r_Sgb], [r_p4])
                S.tt("dve", tmp4[:], p43, bl(eg[:, 0:4]), MUL, [r_p4, r_eg], [r_tmp])
                yield
                p5, r_p5 = psw.next()
                p53 = v3(p5[:, :])
                for h in range(4):
                    S.mm(p53[:, h, :], attT4[:, h, :], uc4[:, h, :], [r_attT, r_uc], [r_p5])
                if d == 0:
                    S.tt("dve", v3(ot[:, 0:512]), p53, tmp4[:], ADD, [r_p5, r_tmp], [r_ot])
                else:
                    S.tt("dve", tmp4[:], p53, tmp4[:], ADD, [r_p5, r_tmp], [r_tmp])
                    S.tt("pool", v3(ot[:, 0:512]), tmp4[:], v3(od0[:, 0:512]), ADD, [r_tmp, r_od0], [r_ot])
                yield
            p3, r_p3 = psw.next()
            p33 = v3(p3[:, :])
            for h in range(4):
                S.mm(p33[:, h, :], kend4[:, h, :], uc4[:, h, :], [r_kend, r_uc], [r_p3])
            S.tt("dve", Sg[:], Sg[:], bl(ge4[:, 4:8]), MUL, [r_Sg, r_ge], [r_Sg])
            S.tt("dve", Sg[:], Sg[:], p33, ADD, [r_Sg, r_p3], [r_Sg])
            S.cp("act", Sgb[:], Sg[:], [r_Sg], [r_Sgb])
            yield

        def hgrn_tile(t, ot, r_ot, od0, r_od0):
            tsl = slice(t * 128, (t + 1) * 128)
            cend = t * 128 + (127 if d == 0 else 0)
            lge4, r_lge = s8.next()
            ke4, r_ke = G["ke"]
            keT4, r_keT = G["keT"]
            kendh4, r_kendh = G["kendh"]
            atb4, r_atb = G["atb"]
            S.cp("act", lge4[:, 0:4].unsqueeze(2), lg[:, :, cend:cend + 1], [r_lg], [r_lge])
            S.act(lge4[:, 4:8], lge4[:, 0:4], AF.Exp, [r_lge], [r_lge])
            S.tt("dve", ke4[:], lg[:, :, tsl], bl(lge4[:, 0:4]), SUB, [r_lg, r_lge], [r_ke])
            S.act(ke4[:], ke4[:], AF.Exp, [r_ke], [r_ke], scale=-1.0)
            S.tt("dve", keT4[:], kTh[:, :, tsl], ke4[:], MUL, [r_kTh, r_ke], [r_keT])
            for h in range(4):
                S.tr(bkb[:, h * 128:(h + 1) * 128], keT4[:, h, :], ident_b, [r_keT, r_cstb], [r_bkb])
            S.cp("dve", kendh4[:], v3(bkb[:, 0:512]), [r_bkb], [r_kendh])
            yield
            if with_out:
                pa, r_pa = psw.next()
                pa3 = v3(pa[:, :])
                for h in range(4):
                    S.mm(pa3[:, h, :], ktT[:, h, tsl], qtT[:, h, tsl], [r_ktT, r_qtT], [r_pa])
                S.tt("dve", atb4[:], pa3, bh(MI), MUL, [r_pa, r_cstb], [r_atb])
                yield
                po, r_po = psw.next()
                po3 = v3(po[:, :])
                for h in range(4):
                    S.mm(po3[:, h, :], qtT[:, h, tsl], Shb[:, h, :], [r_qtT, r_Shb], [r_po], start=True, stop=False)
                    S.mm(po3[:, h, :], atb4[:, h, :], vBtok[:, t, h * 128:(h + 1) * 128], [r_atb, r_vBtok], [r_po], start=False, stop=True)
                if d == 0:
                    S.cp("act", v3(ot[:, 512:1024]), po3, [r_po], [r_ot])
                else:
                    S.tt("dve", v3(ot[:, 512:1024]), po3, v3(od0[:, 512:1024]), ADD, [r_po, r_od0], [r_ot])
                yield
            ps_, r_ps = psw.next()
            ps3 = v3(ps_[:, :])
            for h in range(4):
                S.mm(ps3[:, h, :], kendh4[:, h, :], vBtok[:, t, h * 128:(h + 1) * 128], [r_kendh, r_vBtok], [r_ps])
            S.tt("dve", Sh[:], Sh[:], bl(lge4[:, 4:8]), MUL, [r_Sh, r_lge], [r_Sh])
            S.tt("dve", Sh[:], Sh[:], ps3, ADD, [r_Sh, r_ps], [r_Sh])
            S.cp("act", Shb[:], Sh[:], [r_Sh], [r_Shb])
            yield

        def run_rr(gens):
            gens = list(gens)
            while gens:
                for g_ in list(gens):
                    try:
                        next(g_)
                    except StopIteration:
                        gens.remove(g_)

        run_rr([gdn_stage1(order[0], 0)])
        for oi, t in enumerate(order):
            tl = tile0 + t
            par = oi % 2
            ot = r_ot = od0 = r_od0 = None
            if with_out:
                ot, r_ot = oring.next()
                if d == 1:
                    od0, r_od0 = od0r.next()
                    S.dma(od0[:], st_o[tl], writes=[r_od0])
            gens = [gdn_stage2(t, par, ot, r_ot, od0, r_od0), hgrn_tile(t, ot, r_ot, od0, r_od0)]
            if oi + 1 < len(order):
                gens += [gdn_stage1(order[oi + 1], 1 - par)]
            run_rr(gens)
            if with_out:
                S.dma(st_o[tl], ot[:], reads=[r_ot], writes=[])
    S.end_phase()


_CONSTS = None


def prep_inputs(inp):
    global _CONSTS
    if _CONSTS is None:
        _CONSTS = make_consts()
    f = lambda a: np.ascontiguousarray(np.asarray(a, dtype=np.float32))
    x, c, ctx, c_ctx = f(inp["x"]), f(inp["c"]), f(inp["ctx"]), f(inp["c_ctx"])
    w_in = f(inp["w_in"])[0]
    o = dict(qkv=(0, 1536), a_f=(1536, 1540), a_b=(1540, 1544), be_f=(1544, 1548), be_b=(1548, 1552), g_a=(1552, 2064),
             q_b=(2064, 2576), i_b=(2576, 3088), f_f=(3088, 3600), f_b=(3600, 4112), g_b=(4112, 4624), m_a=(4624, 5648), m_b=(5648, 6672))
    def cols(names):
        return np.concatenate([w_in[:, o[n][0]:o[n][1]] for n in names], axis=1)
    w_even = np.ascontiguousarray(cols(["qkv", "a_f", "a_b", "be_f", "be_b", "q_b", "i_b", "f_f", "f_b", "g_a", "g_b", "m_a", "m_b"]))
    w_odd = np.ascontiguousarray(cols(["qkv", "a_b", "a_f", "be_b", "be_f", "q_b", "i_b", "f_b", "f_f", "g_a", "g_b", "m_a", "m_b"]))
    conv_w = f(inp["conv_w"])[0]
    a_log, dt_bias = f(inp["a_log"])[0], f(inp["dt_bias"])[0]
    shared = {
        "mod_w": f(inp["mod_w"])[0], "mod_b": f(inp["mod_b"])[0].reshape(1, -1),
        "norm_mix_g": f(inp["norm_mix_g"])[0], "norm_ffn_g": f(inp["norm_ffn_g"])[0],
        "gdn_norm_g": f(inp["gdn_norm_g"])[0].reshape(1, 128), "lb_logits": f(inp["lb_logits"]),
        "hgrn_norm_g": f(inp["hgrn_norm_g"])[0].reshape(1, 128),
        "w_up_a": f(inp["w_up_a"])[0], "w_up_b": f(inp["w_up_b"])[0], "w_out": f(inp["w_out"])[0],
        "ffn_w_in": f(inp["ffn_w_in"])[0], "ffn_w_out": f(inp["ffn_w_out"])[0],
        "final_norm_g": f(inp["final_norm_g"]).reshape(1, -1), "consts": _CONSTS,
    }
    maps = []
    for core in range(8):
        b, half = core // 2, core % 2
        xs = x[b, half * NTOK:(half + 1) * NTOK]
        cs = ctx[b]
        m = dict(shared)
        if half:
            xs, cs = xs[::-1], cs[::-1]
            m["w_in"] = w_odd
            m["conv_w"] = np.ascontiguousarray(conv_w[::-1])
            m["a_log"] = np.concatenate([a_log[1], a_log[0]]).reshape(1, 8)
            m["dt_bias"] = np.concatenate([dt_bias[1], dt_bias[0]]).reshape(1, 8)
            m["sel"] = np.tile(np.array([[1.0, 0.0]], np.float32), (128, 1))
        else:
            m["w_in"] = w_even
            m["conv_w"] = conv_w
            m["a_log"] = np.concatenate([a_log[0], a_log[1]]).reshape(1, 8)
            m["dt_bias"] = np.concatenate([dt_bias[0], dt_bias[1]]).reshape(1, 8)
            m["sel"] = np.tile(np.array([[0.0, 1.0]], np.float32), (128, 1))
        m["x"] = np.ascontiguousarray(xs)
        m["ctx"] = np.ascontiguousarray(cs)
        m["cvec"] = np.ascontiguousarray(np.stack([c[b], c_ctx]))
        maps.append(m)
    return maps


def run(inp, stop_after=None, trace=False):
    nc = build_program(stop_after)
    maps = prep_inputs(inp)
    res = run_bass_kernel_spmd(nc, maps, core_ids=list(range(8)), trace=trace)
    outs = []
    for core in range(8):
        o = res.results[core]["out"]
        if core % 2:
            o = o[::-1]
        outs.append(o)
    full = np.stack([np.concatenate([outs[2 * b], outs[2 * b + 1]], axis=0) for b in range(4)])
    return full, res


def kernel(**inputs):
    full, _ = run(inputs)
    return np.ascontiguousarray(full.astype(np.float32))


def exchange(cx):
    import os
    S = cx["S"]
    MUL, ADD = ALU.mult, ALU.add
    Sg, Sh, Sgb, Shb = cx["Sg"], cx["Sh"], cx["Sgb"], cx["Shb"]
    r_Sg, r_Sh, r_Sgb, r_Shb = cx["r_Sg"], cx["r_Sh"], cx["r_Sgb"], cx["r_Shb"]
    ex_in, ex_out, sel, r_sel = cx["ex_in"], cx["ex_out"], cx["sel"], cx["r_sel"]
    S.begin_phase()
    r_i, r_o = Res("exin"), Res("exout")
    S.dma(ex_in[:, 0:512], Sg[:].rearrange("p h v -> p (h v)"), reads=[r_Sg], writes=[r_i])
    S.dma(ex_in[:, 512:1024], Sh[:].rearrange("p h v -> p (h v)"), reads=[r_Sh], writes=[r_i])
    sl, r_sl = S.buf("slots", [128, 2, 1024], F32)
    if os.environ.get("KNOEX"):
        S.dma(sl[:, 0, :], ex_in, reads=[r_i], writes=[r_sl])
        S.dma(sl[:, 1, :], ex_in, reads=[r_i], writes=[r_sl])
    else:
        rg = [[0, 1], [2, 3], [4, 5], [6, 7]]
        S.dma(None, None, reads=[r_i], writes=[r_o], queue="pool", inc=1,
              fn=lambda e: e.collective_compute("AllGather", ALU.bypass, replica_groups=rg, ins=[ex_in.opt()], outs=[ex_out.opt()]))
        S.dma(sl[:], ex_out.rearrange("(r p) n -> p r n", p=128), reads=[r_o], writes=[r_sl])
    tmp, r_tmp = S.buf("extmp", [128, 1024], F32)
    S.ts("dve", tmp[:], sl[:, 0, :], sel[:, 0:1], None, MUL, None, [r_sl, r_sel], [r_tmp])
    S.stt(tmp[:], sl[:, 1, :], sel[:, 1:2], tmp[:], MUL, ADD, [r_sl, r_sel, r_tmp], [r_tmp])
    S.cp("dve", Sg[:].rearrange("p h v -> p (h v)"), tmp[:, 0:512], [r_tmp], [r_Sg])
    S.cp("dve", Sh[:].rearrange("p h v -> p (h v)"), tmp[:, 512:1024], [r_tmp], [r_Sh])
    S.cp("act", Sgb[:].rearrange("p h v -> p (h v)"), tmp[:, 0:512], [r_tmp], [r_Sgb])
    S.cp("act", Shb[:].rearrange("p h v -> p (h v)"), tmp[:, 512:1024], [r_tmp], [r_Shb])
    S.end_phase()


def _norm_hT(S, cx, xt, r_xt, xn, r_xn, st1, hT, r_hT, t, Ai, Bi, psf):
    MUL, ADD, POW = ALU.mult, ALU.add, ALU.pow
    modT, r_modT, sm_c, r_smc = cx["modT"], cx["r_modT"], cx["sm_c"], cx["r_smc"]
    ident_f, r_cst = cx["ident_f"], cx["r_cst"]
    ss, r_ss = st1.next()
    S.act(xn[:], xt[:], AF.Square, [r_xt], [r_xn, r_ss], accum_out=ss[:, 0:1])
    S.ts("dve", ss[:, 1:2], ss[:, 0:1], 1.0 / D, EPS, MUL, ADD, [r_ss], [r_ss])
    S.tt("pool", ss[:, 2:3], ss[:, 1:2], sm_c[:, 0:1], POW, [r_ss, r_smc], [r_ss])
    S.ts("dve", xn[:], xt[:], ss[:, 2:3], None, MUL, None, [r_xt, r_ss], [r_xn])
    for half in range(2):
        pb, r_pb = psf.next()
        for q in range(4):
            k = half * 4 + q
            S.tr(pb[:, q * 128:(q + 1) * 128], xn[:, k * 128:(k + 1) * 128], ident_f, [r_xn, r_cst], [r_pb])
        for q in range(4):
            k = half * 4 + q
            o_ap = hT[:, k, t * 128:(t + 1) * 128]
            i_ap = pb[:, q * 128:(q + 1) * 128]
            if q % 2 == 0:
                S.act(o_ap, i_ap, AF.Identity, [r_pb, r_modT], [r_hT], scale=modT[:, Ai, k:k + 1], bias=modT[:, Bi, k:k + 1])
            else:
                S.ts("dve", o_ap, i_ap, modT[:, Ai, k:k + 1], modT[:, Bi, k:k + 1], MUL, ADD, [r_pb, r_modT], [r_hT])


def dense_c1(cx):
    import os
    S = cx["S"]
    MUL, ADD, POW = ALU.mult, ALU.add, ALU.pow
    psf, psb, bkb, r_bkb = cx["psf"], cx["psb"], cx["bkb"], cx["r_bkb"]
    psw = Ring(list(cx["psw"].bufs) + list(cx["psf"].bufs))
    ident_b, r_cstb = cx["ident_b"], cx["r_cstb"]
    sm_c, r_smc, gvec, r_gvec = cx["sm_c"], cx["r_smc"], cx["gvec"], cx["r_gvec"]
    x_d, win_d, st_o, st_x1 = cx["x_d"], cx["win_d"], cx["st_o"], cx["st_x1"]
    S.begin_phase()
    gbc, r_gbc = S.buf("gbc", [128, 2, D], F32)
    S.dma(gbc[:], cx["st_gbc"], writes=[r_gbc])
    wG, _ = S.buf("wG", [128, 8, 3072], BF16)
    r_wGb = [Res("wG%d" % i) for i in range(6)]
    for i in range(6):
        S.dma(wG[:, :, i * 512:(i + 1) * 512], win_d[:, C_GA + i * 512:C_GA + (i + 1) * 512].rearrange("(k p) n -> p k n", p=128),
              writes=[r_wGb[i]], queue="pool")
    wU, r_wU = S.buf("wU", [128, 8, D], BF16)
    S.dma(wU[:, 0:4, :], cx["wua_d"].rearrange("(k p) n -> p k n", p=128), writes=[r_wU], queue="pool")
    S.dma(wU[:, 4:8, :], cx["wub_d"].rearrange("(k p) n -> p k n", p=128), writes=[r_wU], queue="pool")
    wO, r_wO = S.buf("wO", [128, 8, D], BF16)
    S.dma(wO[:], cx["wout_d"].rearrange("(k p) n -> p k n", p=128), writes=[r_wO], queue="pool")
    xring = S.ring("xt", [128, D], F32, 2)
    xnring = S.ring("xn", [128, D], F32, 1)
    st1 = S.ring("st1", [128, 4], F32, 4)
    hT, r_hT = S.buf("hT", [128, 8, 512], BF16)
    oring = S.ring("ot", [128, D], F32, 2)
    gsr = S.ring("gs", [128, 512], F32, 2)
    f512 = S.ring("f512", [128, 512], F32, 4)
    gnr = S.ring("gn", [128, 128], F32, 3)
    gn2r = S.ring("gn2", [128, 4, 128], BF16, 2)
    st8 = S.ring("st8", [128, 16], F32, 3)
    gnT, r_gnT = S.buf("gnT", [128, 8, 512], BF16)
    sgT, r_sgT = S.buf("sgT", [128, 16, 512], BF16)
    mg, r_mg = S.buf("mg", [128, 8, 512], BF16)
    x1r = S.ring("x1", [128, D], F32, 2)
    ngroups = 8 if not os.environ.get("KLIM") else int(os.environ["KLIM"]) - 1
    for g in range(ngroups):
        tile0 = g * 4
        for t in range(4):
            xt, r_xt = xring.next()
            xn, r_xn = xnring.next()
            S.dma(xt[:], x_d[(tile0 + t) * 128:(tile0 + t + 1) * 128, :], writes=[r_xt])
            _norm_hT(S, cx, xt, r_xt, xn, r_xn, st1, hT, r_hT, t, 0, 1, psf)
        def gen_norm(tile0=tile0):
          for t in range(4):
            tl = tile0 + t
            ot, r_ot = oring.next()
            S.dma(ot[:], st_o[tl], writes=[r_ot])
            st, r_st = st8.next()
            junk, r_junk = f512.next()
            for hh in range(8):
                S.act(junk[:, 0:128], ot[:, hh * 128:(hh + 1) * 128], AF.Square, [r_ot], [r_junk, r_st], accum_out=st[:, hh:hh + 1])
            S.ts("dve", st[:, 8:16], st[:, 0:8], 1.0 / 128, EPS, MUL, ADD, [r_st], [r_st])
            S.tt("pool", st[:, 8:16], st[:, 8:16], sm_c[:, 0:1].to_broadcast([128, 8]), POW, [r_st, r_smc], [r_st])
            for m in range(2):
                pb, r_pb = psw.next()
                for k in range(8):
                    S.mm(pb[:, :], hT[:, k, t * 128:(t + 1) * 128], wG[:, k, m * 512:(m + 1) * 512], [r_hT, r_wGb[m]], [r_pb], start=(k == 0), stop=(k == 7))
                gs, r_gs = gsr.next()
                S.act(gs[:], pb[:, :], AF.Silu, [r_pb], [r_gs])
                o3 = ot[:, m * 512:(m + 1) * 512].rearrange("p (h n) -> p h n", h=4)
                g3 = gs[:].rearrange("p (h n) -> p h n", h=4)
                S.tt("dve", g3, g3, gvec[:, m, :].unsqueeze(1).to_broadcast([128, 4, 128]), MUL, [r_gs, r_gvec], [r_gs])
                S.tt("dve", g3, g3, st[:, 8 + m * 4:12 + m * 4].unsqueeze(2).to_broadcast([128, 4, 128]), MUL, [r_gs, r_st], [r_gs])
                gn2, r_gn2 = gn2r.next()
                S.tt("dve", gn2[:], g3, o3, MUL, [r_gs, r_ot], [r_gn2])
                for h in range(4):
                    S.tr(bkb[:, h * 128:(h + 1) * 128], gn2[:, h, :], ident_b, [r_gn2, r_cstb], [r_bkb])
                S.cp("act", gnT[:, m * 4:(m + 1) * 4, t * 128:(t + 1) * 128], bkb[:, 0:512].rearrange("p (h n) -> p h n", h=4), [r_bkb], [r_gnT])
                yield
        def gen_gate():
          for c in range(16):
            pb, r_pb = psw.next()
            for k in range(8):
                S.mm(pb[:, :], wG[:, k, 1024 + c * 128:1024 + (c + 1) * 128], hT[:, k, :], [r_hT, r_wGb[2 + c // 4]], [r_pb], start=(k == 0), stop=(k == 7))
            th, r_th = f512.next()
            S.act(th[:], pb[:, :], AF.Tanh, [r_pb], [r_th], scale=0.5)
            S.ts("pool" if c % 2 else "dve", sgT[:, c, :], th[:], 0.5, 0.5, MUL, ADD, [r_th], [r_sgT])
            yield
        gens_ = [gen_gate(), gen_norm()]
        while gens_:
            for g_ in list(gens_):
                try:
                    next(g_)
                except StopIteration:
                    gens_.remove(g_)
        for dc in range(8):
            pa, r_pa = psw.next()
            for k in range(4):
                S.mm(pa[:, :], wU[:, k, dc * 128:(dc + 1) * 128], gnT[:, k, :], [r_wU, r_gnT], [r_pa], start=(k == 0), stop=(k == 3))
            pb, r_pb = psw.next()
            for k in range(4):
                S.mm(pb[:, :], wU[:, 4 + k, dc * 128:(dc + 1) * 128], gnT[:, 4 + k, :], [r_wU, r_gnT], [r_pb], start=(k == 0), stop=(k == 3))
            t1, r_t1 = f512.next()
            S.tt("dve", t1[:], pa[:, :], sgT[:, dc, :], MUL, [r_pa, r_sgT], [r_t1])
            t2, r_t2 = f512.next()
            S.tt("dve", t2[:], pb[:, :], sgT[:, 8 + dc, :], MUL, [r_pb, r_sgT], [r_t2])
            S.tt("pool", mg[:, dc, :], t1[:], t2[:], ADD, [r_t1, r_t2], [r_mg])
        for t in range(4):
            tl = tile0 + t
            xt, r_xt = xring.next()
            S.dma(xt[:], x_d[tl * 128:(tl + 1) * 128, :], writes=[r_xt])
            x1, r_x1 = x1r.next()
            for nh in range(2):
                pb, r_pb = psw.next()
                for k in range(8):
                    S.mm(pb[:, :], mg[:, k, t * 128:(t + 1) * 128], wO[:, k, nh * 512:(nh + 1) * 512], [r_mg, r_wO], [r_pb], start=(k == 0), stop=(k == 7))
                tm, r_tm = f512.next()
                S.tt("dve", tm[:], pb[:, :], gbc[:, 0, nh * 512:(nh + 1) * 512], MUL, [r_pb, r_gbc], [r_tm])
                S.tt("pool", x1[:, nh * 512:(nh + 1) * 512], tm[:], xt[:, nh * 512:(nh + 1) * 512], ADD, [r_tm, r_xt], [r_x1])
            S.dma(st_x1[tl], x1[:], reads=[r_x1])
    S.end_phase()


def dense_c2(cx):
    import os
    S = cx["S"]
    MUL, ADD, POW = ALU.mult, ALU.add, ALU.pow
    psw, psf = cx["psw"], cx["psf"]
    sm_c, r_smc = cx["sm_c"], cx["r_smc"]
    st_x1, out_d = cx["st_x1"], cx["out_d"]
    S.begin_phase()
    gbc, r_gbc = S.buf("gbc", [128, D], F32)
    S.dma(gbc[:], cx["st_gbc"][:, 1, :], writes=[r_gbc])
    fWi, _ = S.buf("fWi", [128, 8, 2 * DFF], BF16)
    r_fWg = [Res("fWg%d" % i) for i in range(11)]
    r_fWu = [Res("fWu%d" % i) for i in range(11)]
    for i in range(11):
        for off, rr in ((0, r_fWg), (DFF, r_fWu)):
            S.dma(fWi[:, :, off + i * 256:off + (i + 1) * 256], cx["fwi_d"][:, off + i * 256:off + (i + 1) * 256].rearrange("(k p) n -> p k n", p=128),
                  writes=[rr[i]], queue="pool")
    fWo, _ = S.buf("fWo", [128, 22, D], BF16)
    r_fWoj = [Res("fWo%d" % i) for i in range(11)]
    for j0 in range(0, 22, 2):
        S.dma(fWo[:, j0:j0 + 2, :], cx["fwo_d"][j0 * 128:(j0 + 2) * 128, :].rearrange("(j p) n -> p j n", p=128), writes=[r_fWoj[j0 // 2]], queue="pool")
    fgb, r_fgb = S.buf("fgb", [128, D], F32)
    S.dma(fgb[:], cx["fng_d"].partition_broadcast(128), writes=[r_fgb])
    x1gs = [S.buf("x1g%d" % p_, [128, 2, D], F32) for p_ in range(2)]
    xnring = S.ring("xn", [128, D], F32, 1)
    st1 = S.ring("st1", [128, 4], F32, 4)
    h2Ts = [S.buf("h2T%d" % p_, [128, 8, 256], BF16) for p_ in range(2)]
    aT, r_aT = S.buf("aT", [128, 22, 256], BF16)
    f256 = S.ring("f256", [128, 256], F32, 2)
    f512 = S.ring("f512", [128, 512], F32, 2)
    x2r = S.ring("x2", [128, D], F32, 1)
    ngroups = 16 if not os.environ.get("KLIM") else 2 * (int(os.environ["KLIM"]) - 1)
    def norm_group(g):
        x1g, r_x1g = x1gs[g % 2]
        h2T, r_h2T = h2Ts[g % 2]
        for t in range(2):
            S.dma(x1g[:, t, :], st_x1[g * 2 + t], writes=[r_x1g])
            xn, r_xn = xnring.next()
            _norm_hT(S, cx, x1g[:, t, :], r_x1g, xn, r_xn, st1, h2T, r_h2T, t, 2, 3, psf)

    norm_group(0)
    for g in range(ngroups):
        tile0 = g * 2
        x1g, r_x1g = x1gs[g % 2]
        h2T, r_h2T = h2Ts[g % 2]
        for j in range(22):
            pg, r_pg = psw.next()
            for k in range(8):
                S.mm(pg[:, 0:256], fWi[:, k, j * 128:(j + 1) * 128], h2T[:, k, :], [r_fWg[j // 2], r_h2T], [r_pg], start=(k == 0), stop=(k == 7))
            pu, r_pu = psw.next()
            for k in range(8):
                S.mm(pu[:, 0:256], fWi[:, k, DFF + j * 128:DFF + (j + 1) * 128], h2T[:, k, :], [r_fWu[j // 2], r_h2T], [r_pu], start=(k == 0), stop=(k == 7))
            sg, r_sg = f256.next()
            S.act(sg[:], pg[:, 0:256], AF.Silu, [r_pg], [r_sg])
            S.tt("dve", aT[:, j, :], pu[:, 0:256], sg[:], MUL, [r_pu, r_sg], [r_aT])
        if g + 1 < ngroups:
            norm_group(g + 1)
        for t in range(2):
            tl = tile0 + t
            x2, r_x2 = x2r.next()
            for nh in range(2):
                pb, r_pb = psw.next()
                for j in range(22):
                    S.mm(pb[:, :], aT[:, j, t * 128:(t + 1) * 128], fWo[:, j, nh * 512:(nh + 1) * 512], [r_aT, r_fWoj[j // 2]], [r_pb], start=(j == 0), stop=(j == 21))
                tm, r_tm = f512.next()
                S.tt("dve", tm[:], pb[:, :], gbc[:, nh * 512:(nh + 1) * 512], MUL, [r_pb, r_gbc], [r_tm])
                S.tt("pool", x2[:, nh * 512:(nh + 1) * 512], tm[:], x1g[:, t, nh * 512:(nh + 1) * 512], ADD, [r_tm, r_x1g], [r_x2])
            ss, r_ss = st1.next()
            xn, r_xn = xnring.next()
            S.act(xn[:], x2[:], AF.Square, [r_x2], [r_xn, r_ss], accum_out=ss[:, 0:1])
            S.ts("dve", ss[:, 1:2], ss[:, 0:1], 1.0 / D, EPS, MUL, ADD, [r_ss], [r_ss])
            S.tt("pool", ss[:, 2:3], ss[:, 1:2], sm_c[:, 0:1], POW, [r_ss, r_smc], [r_ss])
            S.stt(x2[:], x2[:], ss[:, 2:3], fgb[:], MUL, MUL, [r_x2, r_ss, r_fgb], [r_x2])
            S.dma(out_d[tl * 128:(tl + 1) * 128, :], x2[:], reads=[r_x2], final=True)
    S.end_phase(last=True)
```

```python
import numpy as np
from contextlib import ExitStack
import concourse.bass as bass
import concourse.mybir as mybir
from concourse.bass_utils import run_bass_kernel_spmd

F32 = mybir.dt.float32
BF16 = mybir.dt.bfloat16
AF = mybir.ActivationFunctionType
ALU = mybir.AluOpType


class Res:
    __slots__ = ("name", "w", "r")

    def __init__(self, name=""):
        self.name = name
        self.w = None
        self.r = {}


class Sched:
    ENG = ("pe", "act", "dve", "pool", "sp")

    def __init__(self, nc, n_dma_sems=33):
        self.nc = nc
        self.es = ExitStack()
        self.prog = {e: [] for e in self.ENG}
        self.cnt = {e: 0 for e in self.ENG}
        self.sem = {e: self.es.enter_context(nc.semaphore("s_" + e)) for e in self.ENG if e != "sp"}
        self.dsem = [self.es.enter_context(nc.semaphore("d%d" % i)) for i in range(n_dma_sems)]
        self.dcnt = [0] * n_dma_sems
        self.dnext = 0
        self.dnext_sw = 0
        self.known = {e: {} for e in self.ENG}
        self.final_tokens = []
        self.npsum = 0
        self.ph = None
        self.nalloc = 0

    def begin_phase(self):
        self.ph = ExitStack()

    def end_phase(self, last=False):
        self.barrier()
        if last:
            self._waits("sp", self.final_tokens)
        self.emit_block()
        self.ph.close()
        self.ph = None
        if last:
            self.es.close()

    def barrier(self):
        toks = [(e, self.cnt[e]) for e in self.ENG if e != "sp" and self.cnt[e] > 0]
        toks += [(("d", i), v) for i, v in enumerate(self.dcnt) if v > 0]
        for e in self.ENG:
            self._waits(e, toks)

    def sbuf(self, name, shape, dtype, persist=False):
        self.nalloc += 1
        st = self.es if (persist or self.ph is None) else self.ph
        return st.enter_context(self.nc.sbuf_tensor("%s_%d" % (name, self.nalloc), list(shape), dtype))

    def buf(self, name, shape, dtype, persist=False):
        return (self.sbuf(name, shape, dtype, persist), Res(name))

    def ring(self, name, shape, dtype, n, persist=False):
        return Ring([self.buf("%s%d" % (name, i), shape, dtype, persist) for i in range(n)])

    def mm(self, out, lhsT, rhs, reads, writes, start=True, stop=True):
        return self.op("pe", lambda e: e.matmul(out, lhsT=lhsT, rhs=rhs, start=start, stop=stop), reads, writes)

    def tr(self, out, in_, ident, reads, writes):
        return self.op("pe", lambda e: e.transpose(out, in_, ident), reads, writes)

    def act(self, out, in_, func, reads, writes, scale=None, bias=None, accum_out=None):
        kw = {}
        if scale is not None:
            kw["scale"] = scale
        if bias is not None:
            kw["bias"] = bias
        if accum_out is not None:
            kw["accum_out"] = accum_out
        return self.op("act", lambda e: e.activation(out=out, in_=in_, func=func, **kw), reads, writes)

    def tt(self, eng, out, in0, in1, op, reads, writes):
        return self.op(eng, lambda e: e.tensor_tensor(out=out, in0=in0, in1=in1, op=op), reads, writes)

    def ts(self, eng, out, in0, s1, s2, op0, op1, reads, writes):
        if op1 is None:
            return self.op(eng, lambda e: e.tensor_scalar(out=out, in0=in0, scalar1=s1, scalar2=None, op0=op0), reads, writes)
        return self.op(eng, lambda e: e.tensor_scalar(out=out, in0=in0, scalar1=s1, scalar2=s2, op0=op0, op1=op1), reads, writes)

    def stt(self, out, in0, scalar, in1, op0, op1, reads, writes):
        return self.op("dve", lambda e: e.scalar_tensor_tensor(out=out, in0=in0, scalar=scalar, in1=in1, op0=op0, op1=op1), reads, writes)

    def cp(self, eng, out, in_, reads, writes):
        if eng == "act":
            return self.op("act", lambda e: e.copy(out=out, in_=in_), reads, writes)
        return self.op(eng, lambda e: e.tensor_copy(out=out, in_=in_), reads, writes)

    def psum(self, name, shape=(128, 512), dtype=F32):
        return self.es.enter_context(self.nc.psum_tensor(name, list(shape), dtype))

    def _semobj(self, key):
        if isinstance(key, tuple):
            return self.dsem[key[1]]
        return self.sem[key]

    def _waits(self, eng, toks):
        best = {}
        for (k, v) in toks:
            if best.get(k, 0) < v:
                best[k] = v
        kn = self.known[eng]
        for k, v in best.items():
            if kn.get(k, 0) >= v:
                continue
            kn[k] = v
            self.prog[eng].append(("wait", k, v))

    def _deps(self, eng, reads, writes):
        toks = []
        for r in reads:
            if r.w is not None:
                if not (r.w[0] == eng and eng == "pe"):
                    toks.append(r.w)
        for r in writes:
            if r.w is not None and not (r.w[0] == eng and eng == "pe"):
                toks.append(r.w)
            for k, t in r.r.items():
                if not (t[0] == eng and eng == "pe"):
                    toks.append(t)
        return toks

    def op(self, eng, fn, reads=(), writes=()):
        self._waits(eng, self._deps(eng, reads, writes))
        idx = self.cnt[eng]
        self.cnt[eng] += 1
        tok = (eng, idx + 1)
        self.prog[eng].append(("op", fn, idx))
        for r in reads:
            r.r[eng] = tok
        for r in writes:
            r.w = tok
            r.r = {}
        return tok

    def dma(self, out, in_, reads=(), writes=(), queue="sp", final=False, fn=None, inc=16):
        n_sw = 8
        if inc != 16:
            i = len(self.dsem) - 1
        elif queue == "pool":
            i = len(self.dsem) - 1 - n_sw + self.dnext_sw
            self.dnext_sw = (self.dnext_sw + 1) % n_sw
        else:
            i = self.dnext
            self.dnext = (self.dnext + 1) % (len(self.dsem) - 1 - n_sw)
        toks = self._deps(("d", i), reads, writes)
        if self.dcnt[i] > 0:
            toks.append((("d", i), self.dcnt[i]))
        self._waits(queue, toks)
        self.dcnt[i] += inc
        tok = (("d", i), self.dcnt[i])
        if fn is not None:
            self.prog[queue].append(("dmafn", fn, inc, i))
        else:
            self.prog[queue].append(("dma", out, in_, i))
        for r in reads:
            r.r[("d", i)] = tok
        for r in writes:
            r.w = tok
            r.r = {}
        if final:
            self.final_tokens.append(tok)
        return tok

    def emit(self):
        self._waits("sp", self.final_tokens)
        self.emit_block()
        self.es.close()

    def emit_block(self):
        nc = self.nc
        engmap = {"pe": "tensor", "act": "scalar", "dve": "vector", "pool": "gpsimd", "sp": "sync"}
        marks = {e: set() for e in self.sem}
        for ename in self.ENG:
            for item in self.prog[ename]:
                if item[0] == "wait" and not isinstance(item[1], tuple):
                    marks[item[1]].add(item[2] - 1)
        if not hasattr(self, "base"):
            self.base = {e: 0 for e in self.sem}
        val = {}
        for e in self.sem:
            c = self.base[e]
            val[e] = {}
            for item in self.prog[e]:
                if item[0] == "op" and item[2] in marks[e]:
                    c += 1
                    val[e][item[2]] = c
            missing = [m for m in marks[e] if m not in val[e]]
            assert not missing, (e, missing[:5])
            self.base[e] = c

        def run(ename, eng):
            for item in self.prog[ename]:
                if item[0] == "wait":
                    if isinstance(item[1], tuple):
                        eng.wait_ge(self._semobj(item[1]), item[2])
                    else:
                        eng.wait_ge(self.sem[item[1]], val[item[1]][item[2] - 1])
                elif item[0] == "op":
                    ins = item[1](eng)
                    if item[2] in marks[ename]:
                        ins.then_inc(self.sem[ename], 1)
                elif item[0] == "dmafn":
                    item[1](eng).then_inc(self.dsem[item[3]], item[2])
                else:
                    eng.dma_start(out=item[1], in_=item[2]).then_inc(self.dsem[item[3]], 16)

        with nc.Block() as block:
            for ename in self.ENG:
                getattr(block, engmap[ename])(lambda e, _n=ename: run(_n, e))
        for ename in self.ENG:
            self.prog[ename] = []


class Ring:
    def __init__(self, bufs):
        self.bufs = bufs
        self.i = 0

    def next(self):
        b = self.bufs[self.i % len(self.bufs)]
        self.i += 1
        return b


D = 1024
NTOK = 4096
NCTX = 256
NT = NTOK // 128
DIN = 6672
DFF = 2816
EPS = 1e-6
C_QKV, C_SM, C_QB, C_IB, C_F0, C_F1, C_GA, C_GB, C_MA, C_MB = 0, 1536, 1552, 2064, 2576, 3088, 3600, 4112, 4624, 5648
NCB = 26


def make_consts():
    c = np.zeros((128, NCB, 128), np.float32)
    j = np.arange(128)[:, None]
    i = np.arange(128)[None, :]
    c[:, 0] = (j == i)
    c[:, 1] = 1.0
    c[:, 2] = (j <= i)
    c[:, 3] = (j >= i)
    c[:, 4] = np.where(i > j, 0.0, -1e30)
    c[:, 5] = np.where(i < j, 0.0, -1e30)
    c[:, 10] = (j <= i)
    c[:, 11] = (j >= i)
    for l in range(7):
        b = 1 << l
        same = (j // (2 * b)) == (i // (2 * b))
        m0 = (j == i) | (same & ((j % (2 * b)) >= b) & ((i % (2 * b)) < b))
        c[:, 12 + l] = m0
        c[:, 19 + l] = m0.T
    r = np.ones((128, 512), np.float32)
    r[:, ::128] = 0.0
    c[:, 6:10] = r.reshape(128, 4, 128)
    return c.reshape(128, NCB * 128)


def build_program(stop_after=None):
    nc = bass.Bass("TRN2", target_bir_lowering=False)

    def din(name, shape):
        return nc.dram_tensor(name, list(shape), F32, kind="ExternalInput").ap()

    x_d = din("x", [NTOK, D])
    ctx_d = din("ctx", [NCTX, D])
    cvec_d = din("cvec", [2, D])
    modw_d = din("mod_w", [D, 6 * D])
    modb_d = din("mod_b", [1, 6 * D])
    nmg_d = din("norm_mix_g", [D])
    nfg_d = din("norm_ffn_g", [D])
    win_d = din("w_in", [D, DIN])
    convw_d = din("conv_w", [5, 1536])
    alog_d = din("a_log", [1, 8])
    dtb_d = din("dt_bias", [1, 8])
    gng_d = din("gdn_norm_g", [1, 128])
    lbl_d = din("lb_logits", [2, 512])
    hng_d = din("hgrn_norm_g", [1, 128])
    wua_d = din("w_up_a", [512, D])
    wub_d = din("w_up_b", [512, D])
    wout_d = din("w_out", [D, D])
    fwi_d = din("ffn_w_in", [D, 2 * DFF])
    fwo_d = din("ffn_w_out", [DFF, D])
    fng_d = din("final_norm_g", [1, D])
    cst_d = din("consts", [128, NCB * 128])
    sel_d = din("sel", [128, 2])
    out_d = nc.dram_tensor("out", [NTOK, D], F32, kind="ExternalOutput").ap()
    st_fm = nc.dram_tensor("st_fm", [NT, 128, 12, 128], BF16).ap()
    st_tm = nc.dram_tensor("st_tm", [NT, 128, 3, 512], BF16).ap()
    st_o = nc.dram_tensor("st_o", [NT, 128, 1024], F32).ap()
    st_x1 = nc.dram_tensor("st_x1", [NT, 128, 1024], F32).ap()
    st_gbc = nc.dram_tensor("st_gbc", [128, 2, 1024], F32).ap()
    ex_in = nc.dram_tensor("ex_in", [128, 1024], F32).ap()
    ex_out = nc.dram_tensor("ex_out", [256, 1024], F32).ap()

    S = Sched(nc)
    MUL, ADD, SUB, POW = ALU.mult, ALU.add, ALU.subtract, ALU.pow

    cst, r_cst = S.buf("cst", [128, 6, 128], F32, True)
    cstb, r_cstb = S.buf("cstb", [128, NCB, 128], BF16, True)
    S.dma(cst[:], cst_d.rearrange("p (b n) -> p b n", n=128)[:, 0:6, :], writes=[r_cst])
    S.dma(cstb[:], cst_d.rearrange("p (b n) -> p b n", n=128), writes=[r_cstb], queue="pool")
    ident_f, ones_f = cst[:, 0, :], cst[:, 1, :]
    ident_b, ones_b = cstb[:, 0, :], cstb[:, 1, :]
    RC = [r_cst, r_cstb]
    Sg, r_Sg = S.buf("Sg", [128, 4, 128], F32, True)
    Sh, r_Sh = S.buf("Sh", [128, 4, 128], F32, True)
    Sgb, r_Sgb = S.buf("Sgb", [128, 4, 128], BF16, True)
    Shb, r_Shb = S.buf("Shb", [128, 4, 128], BF16, True)
    for t_, r_ in ((Sg, r_Sg), (Sh, r_Sh), (Sgb, r_Sgb), (Shb, r_Shb)):
        S.op("pool", lambda e, t_=t_: e.memset(t_[:], 0.0), [], [r_])
    modT, r_modT = S.buf("modT", [128, 8, 8], F32, True)
    sm_c, r_smc = S.buf("sm_c", [128, 64], F32, True)
    sel, r_sel = S.buf("sel", [128, 2], F32, True)
    S.dma(sel[:], sel_d, writes=[r_sel])
    gvec, r_gvec = S.buf("gvec", [128, 2, 128], F32, True)
    S.dma(gvec[:, 0, :], gng_d.partition_broadcast(128), writes=[r_gvec])
    S.dma(gvec[:, 1, :], hng_d.partition_broadcast(128), writes=[r_gvec])

    psq = []
    for i in range(5):
        bk = S.psum("psq%d" % i)
        psq.append((bk[:, 0:128], Res("psq%d" % i)))
        pswl = locals().setdefault("pswl", [])
        pswl.append((bk, psq[-1][1]))
    psw = Ring(pswl)
    psq = Ring(psq)
    psf = Ring([(S.psum("psf%d" % i), Res("psf%d" % i)) for i in range(2)])
    bkb = S.psum("psb", (128, 1024), BF16)
    r_bkb = Res("psb")
    psb = Ring([(bkb[:, q * 256:q * 256 + 128], r_bkb) for q in range(4)])

    S.begin_phase()
    S.op("pool", lambda e: e.memset(sm_c[:, 0:1], -0.5), [], [r_smc])
    S.op("pool", lambda e: e.memset(sm_c[:, 32:33], EPS), [], [r_smc])
    cv, r_cv = S.buf("cv", [128, 2, 8], F32)
    S.dma(None, None, writes=[r_cv],
          fn=lambda e: e.dma_start(out=cv[:], in_=cvec_d.rearrange("v (k p) -> p v k", p=128), allow_slow_non_contiguous=True))
    scv, r_scv = S.buf("scv", [128, 2, 8], F32)
    S.act(scv[:], cv[:], AF.Silu, [r_cv], [r_scv])
    crep, r_crep = S.buf("crep", [128, 2, 8, 128], F32)
    for v in range(2):
        for k in range(8):
            S.ts("pool", crep[:, v, k, :], ones_f, scv[:, v, k:k + 1], None, MUL, None, [r_scv, r_cst], [r_crep])
    mb, r_mb = S.buf("mb", [1, 6 * D], F32)
    S.dma(mb[:], modb_d, writes=[r_mb])
    mwr = S.ring("mw", [128, 8, 512], F32, 2)
    bc, r_bc = S.buf("bc", [128, 6 * D], F32)
    bcc, r_bcc = S.buf("bcc", [128, 2 * D], F32)
    for cb in range(12):
        mw, r_mw = mwr.next()
        S.dma(mw[:], modw_d[:, cb * 512:(cb + 1) * 512].rearrange("(k p) n -> p k n", p=128), writes=[r_mw])
        for v in range(2):
            if v == 1 and cb >= 4:
                continue
            pb, r_pb = psf.next()
            for k in range(8):
                S.mm(pb[:, :], crep[:, v, k, :], mw[:, k, :], [r_crep, r_mw], [r_pb], start=(k == 0), stop=False)
            S.mm(pb[:, :], ones_f[0:1, :], mb[0:1, cb * 512:(cb + 1) * 512], [r_cst, r_mb], [r_pb], start=False, stop=True)
            if v == 0:
                S.cp("act", bc[:, cb * 512:(cb + 1) * 512], pb[:, :], [r_pb], [r_bc])
            else:
                S.cp("dve", bcc[:, cb * 512:(cb + 1) * 512], pb[:, :], [r_pb], [r_bcc])
    S.dma(st_gbc[:, 0, :], bc[:, 2 * D:3 * D], reads=[r_bc])
    S.dma(st_gbc[:, 1, :], bc[:, 5 * D:6 * D], reads=[r_bc])
    mT, r_mT = S.buf("mT", [128, 6, 8], F32)
    pq, r_pq = psq.next()
    for w_, (src, blk) in enumerate(((bc, 0), (bc, 1), (bc, 3), (bc, 4), (bcc, 0), (bcc, 1))):
        for k in range(8):
            S.mm(pq[:, w_ * 8 + k:w_ * 8 + k + 1], src[0:1, blk * D + k * 128: blk * D + (k + 1) * 128], ones_f[0:1, 0:1],
                 [r_bc, r_bcc, r_cst], [r_pq])
    S.cp("dve", mT[:].rearrange("p a b -> p (a b)"), pq[:, 0:48], [r_pq], [r_mT])
    ng, r_ng = S.buf("ng", [128, 2, 8], F32)
    S.dma(None, None, writes=[r_ng], fn=lambda e: e.dma_start(out=ng[:, 0, :], in_=nmg_d.rearrange("(k p) -> p k", p=128), allow_slow_non_contiguous=True))
    S.dma(None, None, writes=[r_ng], fn=lambda e: e.dma_start(out=ng[:, 1, :], in_=nfg_d.rearrange("(k p) -> p k", p=128), allow_slow_non_contiguous=True))
    for dst, sc_i, sh_i, g_i in ((0, 1, 0, 0), (2, 3, 2, 1), (4, 5, 4, 0)):
        S.stt(modT[:, dst, :], mT[:, sc_i, :], 1.0, ng[:, g_i, :], ADD, MUL, [r_mT, r_ng], [r_modT])
        S.cp("dve", modT[:, dst + 1, :], mT[:, sh_i, :], [r_mT], [r_modT])
    lbl, r_lbl = S.buf("lbl", [128, 2, 4], F32)
    S.dma(None, None, writes=[r_lbl], fn=lambda e: e.dma_start(out=lbl[:], in_=lbl_d.rearrange("r (h p) -> p r h", p=128), allow_slow_non_contiguous=True))
    t4, r_t4 = S.buf("t4", [128, 4], F32)
    S.tt("dve", t4[:], lbl[:, 1, :], lbl[:, 0, :], SUB, [r_lbl], [r_t4])
    S.act(t4[:], t4[:], AF.Exp, [r_t4], [r_t4])
    S.ts("dve", t4[:], t4[:], 1.0, None, ADD, None, [r_t4], [r_t4])
    S.op("dve", lambda e: e.reciprocal(out=sm_c[:, 1:5], in_=t4[:]), [r_t4], [r_smc])
    S.ts("dve", sm_c[:, 5:9], sm_c[:, 1:5], -0.5, 0.5, MUL, ADD, [r_smc], [r_smc])
    S.ts("dve", sm_c[:, 9:13], sm_c[:, 1:5], 0.5, 0.5, MUL, ADD, [r_smc], [r_smc])
    S.dma(sm_c[:, 16:24], dtb_d.partition_broadcast(128), writes=[r_smc])
    al, r_al = S.buf("al", [128, 8], F32)
    S.dma(al[:], alog_d.partition_broadcast(128), writes=[r_al])
    S.act(al[:], al[:], AF.Exp, [r_al], [r_al])
    S.ts("dve", sm_c[:, 24:32], al[:], -1.0, None, MUL, None, [r_al], [r_smc])
    S.end_phase(last=(stop_after == "0"))
    if stop_after == "0":
        return nc
    ctxv = dict(locals())
    scan_pass(ctxv, 0)
    if stop_after != "A":
        ctxv = dict(locals())
        exchange(ctxv)
        scan_pass(ctxv, 1)
        dense_c1(ctxv)
        dense_c2(ctxv)
        return nc
    if stop_after == "A":
        S.begin_phase()
        import os
        ntl = NT if not os.environ.get("KLIM") else 4 * (int(os.environ["KLIM"]) - 1)
        for tl in range(ntl):
            S.dma(out_d[tl * 128:(tl + 1) * 128, :], st_o[tl], final=True)
        S.end_phase(last=True)
        return nc
    return nc


def scan_pass(cx, d):
    S = cx["S"]; nc = cx["nc"]
    MUL, ADD, SUB, POW = ALU.mult, ALU.add, ALU.subtract, ALU.pow
    cst, cstb, r_cst, r_cstb = cx["cst"], cx["cstb"], cx["r_cst"], cx["r_cstb"]
    ident_f, ones_f, ident_b, ones_b = cx["ident_f"], cx["ones_f"], cx["ident_b"], cx["ones_b"]
    psq, psf, psb, psw, bkb, r_bkb = cx["psq"], cx["psf"], cx["psb"], cx["psw"], cx["bkb"], cx["r_bkb"]
    Sg, Sh, Sgb, Shb = cx["Sg"], cx["Sh"], cx["Sgb"], cx["Shb"]
    r_Sg, r_Sh, r_Sgb, r_Shb = cx["r_Sg"], cx["r_Sh"], cx["r_Sgb"], cx["r_Shb"]
    modT, r_modT, sm_c, r_smc = cx["modT"], cx["r_modT"], cx["sm_c"], cx["r_smc"]
    x_d, ctx_d, win_d, convw_d = cx["x_d"], cx["ctx_d"], cx["win_d"], cx["convw_d"]
    st_fm, st_tm, st_o = cx["st_fm"], cx["st_tm"], cx["st_o"]
    tri_f = cst[:, 2 + d, :]
    MS = cst[:, 4 + d, :]
    MI = cstb[:, 10 + d, :]
    ML = [cstb[:, 12 + 7 * d + l, :] for l in range(7)]
    rst = cstb[:, 6:10, :]
    S.begin_phase()
    if d == 0:
        NW = 3088
        o_qkv, o_sm, o_qb, o_ib, o_f = 0, 1536, 1552, 2064, 2576
        wA, r_wA = S.buf("wA", [128, 8, NW], BF16)
        for k in range(8):
            S.dma(wA[:, k, :], win_d[k * 128:(k + 1) * 128, 0:NW], writes=[r_wA], queue="pool")
    else:
        NW = 528
        o_sm, o_f = 0, 16
        wA, r_wA = S.buf("wA", [128, 8, NW], BF16)
        for k in range(8):
            S.dma(wA[:, k, 0:16], win_d[k * 128:(k + 1) * 128, C_SM:C_SM + 16], writes=[r_wA], queue="pool")
            S.dma(wA[:, k, 16:528], win_d[k * 128:(k + 1) * 128, C_F1:C_F1 + 512], writes=[r_wA], queue="pool")
    xring = S.ring("xt", [128, D], F32, 1)
    xnring = S.ring("xn", [128, D], F32, 1)
    st1 = S.ring("st1", [128, 4], F32, 4)
    hT, r_hT = S.buf("hT", [128, 8, 512], BF16)
    qkT, r_qkT = S.buf("qkT", [128, 8, 512], BF16)
    ktok, r_ktok = S.buf("ktok", [128, 4, 512], BF16)
    vtok, r_vtok = S.buf("vtok", [128, 4, 512], BF16)
    qBT, r_qBT = S.buf("qBT", [128, 4, 512], BF16)
    vBtok, r_vBtok = S.buf("vBtok", [128, 4, 512], BF16)
    fT, r_fT = S.buf("fT", [128, 4, 512], F32)
    lg, r_lg = S.buf("lg", [128, 4, 512], F32)
    kTh, r_kTh = S.buf("kTh", [128, 4, 512], BF16)
    qtT, r_qtT = S.buf("qtT", [128, 4, 512], BF16)
    ktT, r_ktT = S.buf("ktT", [128, 4, 512], BF16)
    smraw, r_smraw = S.buf("smraw", [128, 4, 16], F32)
    f512 = S.ring("f512", [128, 512], F32, 3)
    b512 = S.ring("b512", [128, 512], BF16, 2)
    svr = S.ring("svA", [128, 4, 8], F32, 6)
    H4 = [128, 4, 128]
    G = {
        "lar": S.buf("lar4", H4, F32), "dti": S.buf("dti4", H4, F32), "tmp": S.buf("tmp4", H4, F32), "ke": S.buf("ke4", H4, F32),
        "negU": S.buf("negU4", H4, BF16), "xy": S.ring("xy4", H4, BF16, 6),
        "Y": [S.buf("Y7_%d" % p_, H4, BF16) for p_ in range(2)],
        "A": [S.buf("A4_%d" % p_, H4, BF16) for p_ in range(2)],
        "K": [S.buf("K4_%d" % p_, H4, BF16) for p_ in range(2)],
        "ge": [S.buf("ge4_%d" % p_, [128, 8], F32) for p_ in range(2)],
        "rp": S.buf("rp4", H4, BF16), "uc": S.buf("uc4", H4, BF16),
        "keT": S.buf("keT4", H4, BF16), "kendh": S.buf("kendh4", H4, BF16), "atb": S.buf("atb4", H4, BF16),
    }
    s8 = S.ring("s8", [128, 8], F32, 8)
    oring = S.ring("ot", [128, D], F32, 2)
    if d == 1:
        od0r = S.ring("od0", [128, D], F32, 2)
    if d == 0:
        ypad, r_ypad = S.buf("ypad", [128, 12, 544], BF16)
        S.op("pool", lambda e: e.memset(ypad[:], 0.0), [], [r_ypad])
        cwT, r_cwT = S.buf("cwT", [128, 5, 12], F32)
        for j in range(5):
            S.dma(None, None, writes=[r_cwT], fn=lambda e, j=j: e.dma_start(
                out=cwT[:, j, :], in_=convw_d[j].rearrange("(c p) -> p c", p=128), allow_slow_non_contiguous=True))
        dg, r_dg = S.buf("dg", [128, 12, 5, 128], BF16)
        for c in range(12):
            for j in range(5):
                S.ts("pool", dg[:, c, j, :], ident_f, cwT[:, j, c:c + 1], None, MUL, None, [r_cwT, r_cst], [r_dg])

    pall = Ring(list(psw.bufs) + list(psf.bufs))
    if d == 0:
        groups = [("ctx", 0, 2)] + [("lat", g * 4, 4) for g in range(8)]
    else:
        groups = [("lat", g * 4, 4) for g in range(7, -1, -1)]
    import os
    if os.environ.get("KLIM"):
        kl = int(os.environ["KLIM"])
        groups = groups[:kl] if d == 0 else [("lat", g * 4, 4) for g in range(kl - 2, -1, -1)]

    for (kind, tile0, T) in groups:
        ntok = T * 128
        lat = kind == "lat"
        src = x_d if lat else ctx_d
        Ai, Bi = (0, 1) if lat else (4, 5)
        for t in range(T):
            xt, r_xt = xring.next()
            xn, r_xn = xnring.next()
            S.dma(xt[:], src[(tile0 + t) * 128:(tile0 + t + 1) * 128, :], writes=[r_xt])
            ss, r_ss = st1.next()
            S.act(xn[:], xt[:], AF.Square, [r_xt], [r_xn, r_ss], accum_out=ss[:, 0:1])
            S.ts("dve", ss[:, 1:2], ss[:, 0:1], 1.0 / D, EPS, MUL, ADD, [r_ss], [r_ss])
            S.tt("pool", ss[:, 2:3], ss[:, 1:2], sm_c[:, 0:1], POW, [r_ss, r_smc], [r_ss])
            S.ts("dve", xn[:], xt[:], ss[:, 2:3], None, MUL, None, [r_xt, r_ss], [r_xn])
            for half in range(2):
                pb, r_pb = pall.next()
                for q in range(4):
                    k = half * 4 + q
                    S.tr(pb[:, q * 128:(q + 1) * 128], xn[:, k * 128:(k + 1) * 128], ident_f, [r_xn, r_cst], [r_pb])
                for q in range(4):
                    k = half * 4 + q
                    o_ap = hT[:, k, t * 128:(t + 1) * 128]
                    i_ap = pb[:, q * 128:(q + 1) * 128]
                    if q % 2 == 0:
                        S.act(o_ap, i_ap, AF.Identity, [r_pb, r_modT], [r_hT], scale=modT[:, Ai, k:k + 1], bias=modT[:, Bi, k:k + 1])
                    else:
                        S.ts("dve", o_ap, i_ap, modT[:, Ai, k:k + 1], modT[:, Bi, k:k + 1], MUL, ADD, [r_pb, r_modT], [r_hT])

        def proj_fm(col0):
            pb, r_pb = pall.next()
            for k in range(8):
                S.mm(pb[:, 0:ntok], wA[:, k, col0:col0 + 128], hT[:, k, 0:ntok], [r_wA, r_hT], [r_pb], start=(k == 0), stop=(k == 7))
            return pb, r_pb

        for t in range(T):
            pq, r_pq = psq.next()
            for k in range(8):
                S.mm(pq[:, 0:16], hT[:, k, t * 128:(t + 1) * 128], wA[:, k, o_sm:o_sm + 16], [r_wA, r_hT], [r_pq], start=(k == 0), stop=(k == 7))
            S.cp("dve", smraw[:, t, :], pq[:, 0:16], [r_pq], [r_smraw])
            if d == 0:
                pb, r_pb = pall.next()
                for k in range(8):
                    S.mm(pb[:, :], hT[:, k, t * 128:(t + 1) * 128], wA[:, k, o_ib:o_ib + 512], [r_wA, r_hT], [r_pb], start=(k == 0), stop=(k == 7))
                S.cp("act", vBtok[:, t, :], pb[:, :], [r_pb], [r_vBtok])
        if d == 0:
            if lat:
                ypv = ypad[:].rearrange("p c (r w) -> p c r w", w=68)
            for c in range(12):
                pb, r_pb = proj_fm(o_qkv + c * 128)
                if lat:
                    S.cp("act" if c % 2 else "dve", ypv[:, c, 0:2 * T, 2:66], pb[:, 0:ntok].rearrange("p (r w) -> p r w", w=64), [r_pb], [r_ypad])
                else:
                    S.cp("act" if c % 2 else "dve", ypad[:, c, 2:2 + ntok], pb[:, 0:ntok], [r_pb], [r_ypad])
            for c in range(12):
                pb, r_pb = pall.next()
                for j in range(5):
                    rhs = ypv[:, c, 0:2 * T, j:j + 64] if lat else ypad[:, c, j:j + ntok]
                    S.mm(pb[:, 0:ntok], dg[:, c, j, :], rhs, [r_dg, r_ypad], [r_pb], start=(j == 0), stop=(j == 4))
                if c < 8:
                    qf, r_qf = f512.next()
                    S.act(qf[:, 0:ntok], pb[:, 0:ntok], AF.Silu, [r_pb], [r_qf])
                    sq, r_sq = b512.next()
                    S.tt("pool", sq[:, 0:ntok], qf[:, 0:ntok], qf[:, 0:ntok], MUL, [r_qf], [r_sq])
                    p2, r_p2 = pall.next()
                    S.mm(p2[:, 0:ntok], ones_b, sq[:, 0:ntok], [r_cstb, r_sq], [r_p2])
                    rk, r_rk = f512.next()
                    S.act(rk[:, 0:ntok], p2[:, 0:ntok], AF.Ln, [r_p2], [r_rk], bias=sm_c[:, 32:33])
                    S.act(rk[:, 0:ntok], rk[:, 0:ntok], AF.Exp, [r_rk], [r_rk], scale=-0.5)
                    S.stt(qkT[:, c, 0:ntok], qf[:, 0:ntok], (128.0 ** -0.5) if c < 4 else 1.0, rk[:, 0:ntok], MUL, MUL, [r_qf, r_rk], [r_qkT])
                    if c >= 4:
                        for t in range(T):
                            S.tr(bkb[:, t * 128:(t + 1) * 128], qkT[:, c, t * 128:(t + 1) * 128], ident_b, [r_qkT, r_cstb], [r_bkb])
                        S.cp("act", ktok[:, 0:T, (c - 4) * 128:(c - 3) * 128], bkb[:, 0:ntok].rearrange("p (t n) -> p t n", n=128), [r_bkb], [r_ktok])
                else:
                    vT, r_vT = b512.next()
                    S.act(vT[:, 0:ntok], pb[:, 0:ntok], AF.Silu, [r_pb], [r_vT])
                    for t in range(T):
                        S.tr(bkb[:, t * 128:(t + 1) * 128], vT[:, t * 128:(t + 1) * 128], ident_b, [r_vT, r_cstb], [r_bkb])
                    S.cp("dve", vtok[:, 0:T, (c - 8) * 128:(c - 7) * 128], bkb[:, 0:ntok].rearrange("p (t n) -> p t n", n=128), [r_bkb], [r_vtok])
            for h in range(4):
                pb, r_pb = proj_fm(o_qb + h * 128)
                S.act(qBT[:, h, 0:ntok], pb[:, 0:ntok], AF.Silu, [r_pb], [r_qBT])
            if not lat:
                S.op("pool", lambda e: e.memset(ypad[:], 0.0), [], [r_ypad])
        else:
            for t in range(T):
                tl = tile0 + t
                S.dma(qkT[:, :, t * 128:(t + 1) * 128], st_fm[tl, :, 0:8, :], writes=[r_qkT])
                S.dma(qBT[:, :, t * 128:(t + 1) * 128], st_fm[tl, :, 8:12, :], writes=[r_qBT])
                S.dma(ktok[:, t, :], st_tm[tl, :, 0, :], writes=[r_ktok])
                S.dma(vtok[:, t, :], st_tm[tl, :, 1, :], writes=[r_vtok])
                S.dma(vBtok[:, t, :], st_tm[tl, :, 2, :], writes=[r_vBtok])
        for h in range(4):
            pb, r_pb = proj_fm(o_f + h * 128)
            th, r_th = f512.next()
            S.act(th[:, 0:ntok], pb[:, 0:ntok], AF.Tanh, [r_pb], [r_th], scale=0.5)
            S.ts("dve", fT[:, h, 0:ntok], th[:, 0:ntok], sm_c[:, 5 + h:6 + h], sm_c[:, 9 + h:10 + h], MUL, ADD, [r_th, r_smc], [r_fT])
            S.ts("pool", kTh[:, h, 0:ntok], fT[:, h, 0:ntok], -1.0, 1.0, MUL, ADD, [r_fT], [r_kTh])
        if d == 0 and lat:
            for t in range(T):
                tl = tile0 + t
                S.dma(st_fm[tl, :, 0:8, :], qkT[:, :, t * 128:(t + 1) * 128], reads=[r_qkT])
                S.dma(st_fm[tl, :, 8:12, :], qBT[:, :, t * 128:(t + 1) * 128], reads=[r_qBT])
                S.dma(st_tm[tl, :, 0, :], ktok[:, t, :], reads=[r_ktok])
                S.dma(st_tm[tl, :, 1, :], vtok[:, t, :], reads=[r_vtok])
                S.dma(st_tm[tl, :, 2, :], vBtok[:, t, :], reads=[r_vBtok])
        for h in range(4):
            S.act(fT[:, h, 0:ntok], fT[:, h, 0:ntok], AF.Ln, [r_fT], [r_fT])
            S.op("dve", lambda e, h=h, ntok=ntok, T=T: e.tensor_tensor_scan(out=lg[:, h, 0:ntok], data0=rst[:, 0:T, :].rearrange("p a b -> p (a b)"),
                                                            data1=fT[:, h, 0:ntok], initial=0.0, op0=MUL, op1=ADD), [r_fT, r_cstb], [r_lg])
            if d == 1:
                S.stt(lg[:, h, 0:ntok], lg[:, h, 0:ntok], -1.0, fT[:, h, 0:ntok], MUL, ADD, [r_lg, r_fT], [r_lg])
                tot, r_tot = s8.next()
                for t in range(T):
                    c1_ = t * 128 + 127
                    S.tt("dve", tot[:, t:t + 1], fT[:, h, c1_:c1_ + 1], lg[:, h, c1_:c1_ + 1], SUB, [r_fT, r_lg], [r_tot])
                for t in range(T):
                    S.ts("dve", lg[:, h, t * 128:(t + 1) * 128], lg[:, h, t * 128:(t + 1) * 128], tot[:, t:t + 1], None, ADD, None, [r_lg, r_tot], [r_lg])
            e1, r_e1 = f512.next()
            S.act(e1[:, 0:ntok], lg[:, h, 0:ntok], AF.Exp, [r_lg], [r_e1])
            S.tt("dve", qtT[:, h, 0:ntok], qBT[:, h, 0:ntok], e1[:, 0:ntok], MUL, [r_qBT, r_e1], [r_qtT])
            e2, r_e2 = f512.next()
            S.act(e2[:, 0:ntok], lg[:, h, 0:ntok], AF.Exp, [r_lg], [r_e2], scale=-1.0)
            S.tt("pool", ktT[:, h, 0:ntok], kTh[:, h, 0:ntok], e2[:, 0:ntok], MUL, [r_kTh, r_e2], [r_ktT])

        order = list(range(T)) if d == 0 else list(range(T - 1, -1, -1))
        with_out = lat
        svA, r_sv = svr.next()
        gvA, r_gv = svr.next()
        egA, r_eg = svr.next()
        TT = slice(0, T)
        def bt4(ap4):
            return ap4.unsqueeze(1).to_broadcast([128, T, 4])
        S.tt("dve", svA[:, TT, 0:4], smraw[:, TT, 4 * d:4 * d + 4], bt4(sm_c[:, 16 + 4 * d:20 + 4 * d]), ADD, [r_smraw, r_smc], [r_sv])
        S.act(svA[:, TT, 0:4], svA[:, TT, 0:4], AF.Exp, [r_sv], [r_sv])
        S.act(svA[:, TT, 0:4], svA[:, TT, 0:4], AF.Ln, [r_sv], [r_sv], bias=1.0)
        S.tt("dve", svA[:, TT, 0:4], svA[:, TT, 0:4], bt4(sm_c[:, 24 + 4 * d:28 + 4 * d]), MUL, [r_sv, r_smc], [r_sv])
        S.act(svA[:, TT, 4:8], smraw[:, TT, 8 + 4 * d:12 + 4 * d], AF.Exp, [r_smraw], [r_sv], scale=-1.0)
        S.ts("dve", svA[:, TT, 4:8], svA[:, TT, 4:8], 1.0, None, ADD, None, [r_sv], [r_sv])
        S.op("dve", lambda e, svA=svA, TT=TT: e.reciprocal(out=svA[:, TT, 4:8], in_=svA[:, TT, 4:8]), [r_sv], [r_sv])
        pq, r_pq = psq.next()
        for t in range(T):
            S.mm(pq[:, t * 4:(t + 1) * 4], tri_f, svA[:, t, 0:4], [r_cst, r_sv], [r_pq])
        pq3 = pq[:, 0:4 * T].rearrange("p (t n) -> p t n", n=4)
        S.ts("dve", gvA[:, TT, 0:4], pq3, -1.0, None, MUL, None, [r_pq], [r_gv])
        S.act(egA[:, TT, 0:4], pq3, AF.Exp, [r_pq], [r_eg])
        S.ts("dve", gvA[:, TT, 4:8], egA[:, TT, 0:4], -1.0, None, MUL, None, [r_eg], [r_gv])
        S.ts("dve", egA[:, TT, 4:8], svA[:, TT, 4:8], -1.0, None, MUL, None, [r_sv], [r_eg])

        def bh(ap2):
            return ap2.unsqueeze(1).to_broadcast([128, 4, 128])

        def bl(ap4):
            return ap4.unsqueeze(2).to_broadcast([128, 4, 128])

        def v3(ap):
            return ap.rearrange("p (h n) -> p h n", h=4)

        def gdn_stage1(t, par):
            tsl = slice(t * 128, (t + 1) * 128)
            sv, gv, eg = svA[:, t, :], gvA[:, t, :], egA[:, t, :]
            ge4, r_ge = G["ge"][par]
            lar4, r_lar = G["lar"]
            dti4, r_dti = G["dti"]
            negU4, r_negU = G["negU"]
            S.tt("dve", lar4[:], bl(sv[:, 0:4]), bh(ones_f), MUL, [r_sv, r_cst], [r_lar])
            pg, r_pg = psw.next()
            pg3 = v3(pg[:, :])
            for h in range(4):
                S.mm(pg3[:, h, :], lar4[:, h, :], tri_f, [r_lar, r_cst], [r_pg])
            cg = 127 if d == 0 else 0
            S.cp("act", ge4[:, 0:4].unsqueeze(2), pg3[:, :, cg:cg + 1], [r_pg], [r_ge])
            S.act(ge4[:, 4:8], ge4[:, 0:4], AF.Exp, [r_ge], [r_ge])
            S.tt("dve", lar4[:], pg3, bh(MS), ADD, [r_pg, r_cst], [r_lar])
            yield
            S.tt("dve", lar4[:], lar4[:], bl(gv[:, 0:4]), ADD, [r_lar, r_gv], [r_lar])
            S.act(lar4[:], lar4[:], AF.Exp, [r_lar], [r_lar])
            if with_out:
                S.tt("dve", dti4[:], lar4[:], bh(ident_f), ADD, [r_lar, r_cst], [r_dti])
            pk, r_pk = psw.next()
            pk3 = v3(pk[:, :])
            for h in range(4):
                S.mm(pk3[:, h, :], qkT[:, 4 + h, tsl], qkT[:, 4 + h, tsl], [r_qkT], [r_pk])
            S.tt("dve", lar4[:], pk3, lar4[:], MUL, [r_pk, r_lar, r_dti], [r_lar])
            S.tt("dve", negU4[:], lar4[:], bl(eg[:, 4:8]), MUL, [r_lar, r_eg], [r_negU])
            yield
            if with_out:
                pa, r_pa = psw.next()
                pa3 = v3(pa[:, :])
                for h in range(4):
                    S.mm(pa3[:, h, :], qkT[:, 4 + h, tsl], qkT[:, h, tsl], [r_qkT], [r_pa])
                attT4, r_attT = G["A"][par]
                S.tt("dve", attT4[:], pa3, dti4[:], MUL, [r_pa, r_dti], [r_attT])
                yield
            ks4, r_ks = s8.next()
            S.tt("dve", ks4[:, 0:4], gv[:, 0:4], ge4[:, 0:4], ADD, [r_gv, r_ge], [r_ks])
            S.act(ks4[:, 0:4], ks4[:, 0:4], AF.Exp, [r_ks], [r_ks])
            kend4, r_kend = G["K"][par]
            S.tt("dve", kend4[:], v3(ktok[:, t, :]), bl(ks4[:, 0:4]), MUL, [r_ktok, r_ks], [r_kend])
            Xc = Yc = None
            r_X = r_Y = r_cstb
            for l in range(7):
                pm, r_pm = psw.next()
                pm3 = v3(pm[:, :])
                for h in range(4):
                    S.mm(pm3[:, h, :], ident_b, ident_b, [r_cstb], [r_pm], start=True, stop=False)
                    S.mm(pm3[:, h, :], negU4[:, h, :], ident_b if Xc is None else Xc[:, h, :], [r_negU, r_X], [r_pm], start=False, stop=True)
                Mp, r_Mp = G["xy"].next()
                S.tt("dve", Mp[:], pm3, bh(ML[l]), MUL, [r_pm, r_cstb], [r_Mp])
                yield
                if l < 6:
                    if l == 0:
                        Xn, r_Xn = Mp, r_Mp
                    else:
                        px, r_px = psw.next()
                        px3 = v3(px[:, :])
                        for h in range(4):
                            S.mm(px3[:, h, :], Yc[:, h, :], Mp[:, h, :], [r_Y, r_Mp], [r_px])
                        Xn, r_Xn = G["xy"].next()
                        S.cp("act", Xn[:], px3, [r_px], [r_Xn])
                py, r_py = psw.next()
                py3 = v3(py[:, :])
                for h in range(4):
                    S.mm(py3[:, h, :], Mp[:, h, :], ident_b if Yc is None else Yc[:, h, :], [r_Mp, r_Y], [r_py])
                if l < 6:
                    Yn, r_Yn = G["xy"].next()
                else:
                    Yn, r_Yn = G["Y"][par]
                S.cp("act" if l % 2 else "dve", Yn[:], py3, [r_py], [r_Yn])
                yield
                if l < 6:
                    Xc, r_X = Xn, r_Xn
                Yc, r_Y = Yn, r_Yn

        def gdn_stage2(t, par, ot, r_ot, od0, r_od0):
            tsl = slice(t * 128, (t + 1) * 128)
            sv, gv, eg = svA[:, t, :], gvA[:, t, :], egA[:, t, :]
            ge4, r_ge = G["ge"][par]
            Y7, r_Y7 = G["Y"][par]
            kend4, r_kend = G["K"][par]
            tmp4, r_tmp = G["tmp"]
            rp4, r_rp = G["rp"]
            uc4, r_uc = G["uc"]
            p1, r_p1 = psw.next()
            p13 = v3(p1[:, :])
            for h in range(4):
                S.mm(p13[:, h, :], qkT[:, 4 + h, tsl], Sgb[:, h, :], [r_qkT, r_Sgb], [r_p1])
            S.tt("dve", tmp4[:], p13, bl(gv[:, 4:8]), MUL, [r_p1, r_gv], [r_tmp])
            S.tt("dve", rp4[:], tmp4[:], v3(vtok[:, t, :]), ADD, [r_tmp, r_vtok], [r_rp])
            yield
            p2, r_p2 = psw.next()
            p23 = v3(p2[:, :])
            for h in range(4):
                S.mm(p23[:, h, :], Y7[:, h, :], rp4[:, h, :], [r_Y7, r_rp], [r_p2])
            S.tt("dve", uc4[:], p23, bl(sv[:, 4:8]), MUL, [r_p2, r_sv], [r_uc])
            yield
            if with_out:
                attT4, r_attT = G["A"][par]
                p4, r_p4 = psw.next()
                p43 = v3(p4[:, :])
                for h in range(4):
                    S.mm(p43[:, h, :], qkT[:, h, tsl], Sgb[:, h, :], [r_qkT, r_Sgb], [r_p4])
                S.tt("dve", tmp4[:], p43, bl(eg[:, 0:4]), MUL, [r_p4, r_eg], [r_tmp])
                yield
                p5, r_p5 = psw.next()
                p53 = v3(p5[:, :])
                for h in range(4):
                    S.mm(p53[:, h, :], attT4[:, h, :], uc4[:, h, :], [r_attT, r_uc], [r_p5])
                if d == 0:
                    S.tt("dve", v3(ot[:, 0:512]), p53, tmp4[:], ADD, [r_p5, r_tmp], [r_ot])
                else:
                    S.tt("dve", tmp4[:], p53, tmp4[:], ADD, [r_p5, r_tmp], [r_tmp])
                    S.tt("pool", v3(ot[:, 0:512]), tmp4[:], v3(od0[:, 0:512]), ADD, [r_tmp, r_od0], [r_ot])
                yield
            p3, r_p3 = psw.next()
            p33 = v3(p3[:, :])
            for h in range(4):
                S.mm(p33[:, h, :], kend4[:, h, :], uc4[:, h, :], [r_kend, r_uc], [r_p3])
            S.tt("dve", Sg[:], Sg[:], bl(ge4[:, 4:8]), MUL, [r_Sg, r_ge], [r_Sg])
            S.tt("dve", Sg[:], Sg[:], p33, ADD, [r_Sg, r_p3], [r_Sg])
            S.cp("act", Sgb[:], Sg[:], [r_Sg], [r_Sgb])
            yield

        def hgrn_tile(t, ot, r_ot, od0, r_od0):
            tsl = slice(t * 128, (t + 1) * 128)
            cend = t * 128 + (127 if d == 0 else 0)
            lge4, r_lge = s8.next()
            ke4, r_ke = G["ke"]
            keT4, r_keT = G["keT"]
            kendh4, r_kendh = G["kendh"]
            atb4, r_atb = G["atb"]
            S.cp("act", lge4[:, 0:4].unsqueeze(2), lg[:, :, cend:cend + 1], [r_lg], [r_lge])
            S.act(lge4[:, 4:8], lge4[:, 0:4], AF.Exp, [r_lge], [r_lge])
            S.tt("dve", ke4[:], lg[:, :, tsl], bl(lge4[:, 0:4]), SUB, [r_lg, r_lge], [r_ke])
            S.act(ke4[:], ke4[:], AF.Exp, [r_ke], [r_ke], scale=-1.0)
            S.tt("dve", keT4[:], kTh[:, :, tsl], ke4[:], MUL, [r_kTh, r_ke], [r_keT])
            for h in range(4):
                S.tr(bkb[:, h * 128:(h + 1) * 128], keT4[:, h, :], ident_b, [r_keT, r_cstb], [r_bkb])
            S.cp("dve", kendh4[:], v3(bkb[:, 0:512]), [r_bkb], [r_kendh])
            yield
            if with_out:
                pa, r_pa = psw.next()
                pa3 = v3(pa[:, :])
                for h in range(4):
                    S.mm(pa3[:, h, :], ktT[:, h, tsl], qtT[:, h, tsl], [r_ktT, r_qtT], [r_pa])
                S.tt("dve", atb4[:], pa3, bh(MI), MUL, [r_pa, r_cstb], [r_atb])
                yield
                po, r_po = psw.next()
                po3 = v3(po[:, :])
                for h in range(4):
                    S.mm(po3[:, h, :], qtT[:, h, tsl], Shb[:, h, :], [r_qtT, r_Shb], [r_po], start=True, stop=False)
                    S.mm(po3[:, h, :], atb4[:, h, :], vBtok[:, t, h * 128:(h + 1) * 128], [r_atb, r_vBtok], [r_po], start=False, stop=True)
                if d == 0:
                    S.cp("act", v3(ot[:, 512:1024]), po3, [r_po], [r_ot])
                else:
                    S.tt("dve", v3(ot[:, 512:1024]), po3, v3(od0[:, 512:1024]), ADD, [r_po, r_od0], [r_ot])
                yield
            ps_, r_ps = psw.next()
            ps3 = v3(ps_[:, :])
            for h in range(4):
                S.mm(ps3[:, h, :], kendh4[:, h, :], vBtok[:, t, h * 128:(h + 1) * 128], [r_kendh, r_vBtok], [r_ps])
            S.tt("dve", Sh[:], Sh[:], bl(lge4[:, 4:8]), MUL, [r_Sh, r_lge], [r_Sh])
            S.tt("dve", Sh[:], Sh[:], ps3, ADD, [r_Sh, r_ps], [r_Sh])
            S.cp("act", Shb[:], Sh[:], [r_Sh], [r_Shb])
            yield

        def run_rr(gens):
            gens = list(gens)
            while gens:
                for g_ in list(gens):
                    try:
                        next(g_)
                    except StopIteration:
                        gens.remove(g_)

        run_rr([gdn_stage1(order[0], 0)])
        for oi, t in enumerate(order):
            tl = tile0 + t
            par = oi % 2
            ot = r_ot = od0 = r_od0 = None
            if with_out:
                ot, r_ot = oring.next()
                if d == 1:
                    od0, r_od0 = od0r.next()
                    S.dma(od0[:], st_o[tl], writes=[r_od0])
            gens = [gdn_stage2(t, par, ot, r_ot, od0, r_od0), hgrn_tile(t, ot, r_ot, od0, r_od0)]
            if oi + 1 < len(order):
                gens += [gdn_stage1(order[oi + 1], 1 - par)]
            run_rr(gens)
            if with_out:
                S.dma(st_o[tl], ot[:], reads=[r_ot], writes=[])
    S.end_phase()


_CONSTS = None


def prep_inputs(inp):
    global _CONSTS
    if _CONSTS is None:
        _CONSTS = make_consts()
    f = lambda a: np.ascontiguousarray(np.asarray(a, dtype=np.float32))
    x, c, ctx, c_ctx = f(inp["x"]), f(inp["c"]), f(inp["ctx"]), f(inp["c_ctx"])
    w_in = f(inp["w_in"])[0]
    o = dict(qkv=(0, 1536), a_f=(1536, 1540), a_b=(1540, 1544), be_f=(1544, 1548), be_b=(1548, 1552), g_a=(1552, 2064),
             q_b=(2064, 2576), i_b=(2576, 3088), f_f=(3088, 3600), f_b=(3600, 4112), g_b=(4112, 4624), m_a=(4624, 5648), m_b=(5648, 6672))
    def cols(names):
        return np.concatenate([w_in[:, o[n][0]:o[n][1]] for n in names], axis=1)
    w_even = np.ascontiguousarray(cols(["qkv", "a_f", "a_b", "be_f", "be_b", "q_b", "i_b", "f_f", "f_b", "g_a", "g_b", "m_a", "m_b"]))
    w_odd = np.ascontiguousarray(cols(["qkv", "a_b", "a_f", "be_b", "be_f", "q_b", "i_b", "f_b", "f_f", "g_a", "g_b", "m_a", "m_b"]))
    conv_w = f(inp["conv_w"])[0]
    a_log, dt_bias = f(inp["a_log"])[0], f(inp["dt_bias"])[0]
    shared = {
        "mod_w": f(inp["mod_w"])[0], "mod_b": f(inp["mod_b"])[0].reshape(1, -1),
        "norm_mix_g": f(inp["norm_mix_g"])[0], "norm_ffn_g": f(inp["norm_ffn_g"])[0],
        "gdn_norm_g": f(inp["gdn_norm_g"])[0].reshape(1, 128), "lb_logits": f(inp["lb_logits"]),
        "hgrn_norm_g": f(inp["hgrn_norm_g"])[0].reshape(1, 128),
        "w_up_a": f(inp["w_up_a"])[0], "w_up_b": f(inp["w_up_b"])[0], "w_out": f(inp["w_out"])[0],
        "ffn_w_in": f(inp["ffn_w_in"])[0], "ffn_w_out": f(inp["ffn_w_out"])[0],
        "final_norm_g": f(inp["final_norm_g"]).reshape(1, -1), "consts": _CONSTS,
    }
    maps = []
    for core in range(8):
        b, half = core // 2, core % 2
        xs = x[b, half * NTOK:(half + 1) * NTOK]
        cs = ctx[b]
        m = dict(shared)
        if half:
            xs, cs = xs[::-1], cs[::-1]
            m["w_in"] = w_odd
            m["conv_w"] = np.ascontiguousarray(conv_w[::-1])
            m["a_log"] = np.concatenate([a_log[1], a_log[0]]).reshape(1, 8)
            m["dt_bias"] = np.concatenate([dt_bias[1], dt_bias[0]]).reshape(1, 8)
            m["sel"] = np.tile(np.array([[1.0, 0.0]], np.float32), (128, 1))
        else:
            m["w_in"] = w_even
            m["conv_w"] = conv_w
            m["a_log"] = np.concatenate([a_log[0], a_log[1]]).reshape(1, 8)
            m["dt_bias"] = np.concatenate([dt_bias[0], dt_bias[1]]).reshape(1, 8)
            m["sel"] = np.tile(np.array([[0.0, 1.0]], np.float32), (128, 1))
        m["x"] = np.ascontiguousarray(xs)
        m["ctx"] = np.ascontiguousarray(cs)
        m["cvec"] = np.ascontiguousarray(np.stack([c[b], c_ctx]))
        maps.append(m)
    return maps


def run(inp, stop_after=None, trace=False):
    nc = build_program(stop_after)
    maps = prep_inputs(inp)
    res = run_bass_kernel_spmd(nc, maps, core_ids=list(range(8)), trace=trace)
    outs = []
    for core in range(8):
        o = res.results[core]["out"]
        if core % 2:
            o = o[::-1]
        outs.append(o)
    full = np.stack([np.concatenate([outs[2 * b], outs[2 * b + 1]], axis=0) for b in range(4)])
    return full, res


def kernel(**inputs):
    full, _ = run(inputs)
    return np.ascontiguousarray(full.astype(np.float32))


def exchange(cx):
    import os
    S = cx["S"]
    MUL, ADD = ALU.mult, ALU.add
    Sg, Sh, Sgb, Shb = cx["Sg"], cx["Sh"], cx["Sgb"], cx["Shb"]
    r_Sg, r_Sh, r_Sgb, r_Shb = cx["r_Sg"], cx["r_Sh"], cx["r_Sgb"], cx["r_Shb"]
    ex_in, ex_out, sel, r_sel = cx["ex_in"], cx["ex_out"], cx["sel"], cx["r_sel"]
    S.begin_phase()
    r_i, r_o = Res("exin"), Res("exout")
    S.dma(ex_in[:, 0:512], Sg[:].rearrange("p h v -> p (h v)"), reads=[r_Sg], writes=[r_i])
    S.dma(ex_in[:, 512:1024], Sh[:].rearrange("p h v -> p (h v)"), reads=[r_Sh], writes=[r_i])
    sl, r_sl = S.buf("slots", [128, 2, 1024], F32)
    if os.environ.get("KNOEX"):
        S.dma(sl[:, 0, :], ex_in, reads=[r_i], writes=[r_sl])
        S.dma(sl[:, 1, :], ex_in, reads=[r_i], writes=[r_sl])
    else:
        rg = [[0, 1], [2, 3], [4, 5], [6, 7]]
        S.dma(None, None, reads=[r_i], writes=[r_o], queue="pool", inc=1,
              fn=lambda e: e.collective_compute("AllGather", ALU.bypass, replica_groups=rg, ins=[ex_in.opt()], outs=[ex_out.opt()]))
        S.dma(sl[:], ex_out.rearrange("(r p) n -> p r n", p=128), reads=[r_o], writes=[r_sl])
    tmp, r_tmp = S.buf("extmp", [128, 1024], F32)
    S.ts("dve", tmp[:], sl[:, 0, :], sel[:, 0:1], None, MUL, None, [r_sl, r_sel], [r_tmp])
    S.stt(tmp[:], sl[:, 1, :], sel[:, 1:2], tmp[:], MUL, ADD, [r_sl, r_sel, r_tmp], [r_tmp])
    S.cp("dve", Sg[:].rearrange("p h v -> p (h v)"), tmp[:, 0:512], [r_tmp], [r_Sg])
    S.cp("dve", Sh[:].rearrange("p h v -> p (h v)"), tmp[:, 512:1024], [r_tmp], [r_Sh])
    S.cp("act", Sgb[:].rearrange("p h v -> p (h v)"), tmp[:, 0:512], [r_tmp], [r_Sgb])
    S.cp("act", Shb[:].rearrange("p h v -> p (h v)"), tmp[:, 512:1024], [r_tmp], [r_Shb])
    S.end_phase()


def _norm_hT(S, cx, xt, r_xt, xn, r_xn, st1, hT, r_hT, t, Ai, Bi, psf):
    MUL, ADD, POW = ALU.mult, ALU.add, ALU.pow
    modT, r_modT, sm_c, r_smc = cx["modT"], cx["r_modT"], cx["sm_c"], cx["r_smc"]
    ident_f, r_cst = cx["ident_f"], cx["r_cst"]
    ss, r_ss = st1.next()
    S.act(xn[:], xt[:], AF.Square, [r_xt], [r_xn, r_ss], accum_out=ss[:, 0:1])
    S.ts("dve", ss[:, 1:2], ss[:, 0:1], 1.0 / D, EPS, MUL, ADD, [r_ss], [r_ss])
    S.tt("pool", ss[:, 2:3], ss[:, 1:2], sm_c[:, 0:1], POW, [r_ss, r_smc], [r_ss])
    S.ts("dve", xn[:], xt[:], ss[:, 2:3], None, MUL, None, [r_xt, r_ss], [r_xn])
    for half in range(2):
        pb, r_pb = psf.next()
        for q in range(4):
            k = half * 4 + q
            S.tr(pb[:, q * 128:(q + 1) * 128], xn[:, k * 128:(k + 1) * 128], ident_f, [r_xn, r_cst], [r_pb])
        for q in range(4):
            k = half * 4 + q
            o_ap = hT[:, k, t * 128:(t + 1) * 128]
            i_ap = pb[:, q * 128:(q + 1) * 128]
            if q % 2 == 0:
                S.act(o_ap, i_ap, AF.Identity, [r_pb, r_modT], [r_hT], scale=modT[:, Ai, k:k + 1], bias=modT[:, Bi, k:k + 1])
            else:
                S.ts("dve", o_ap, i_ap, modT[:, Ai, k:k + 1], modT[:, Bi, k:k + 1], MUL, ADD, [r_pb, r_modT], [r_hT])


def dense_c1(cx):
    import os
    S = cx["S"]
    MUL, ADD, POW = ALU.mult, ALU.add, ALU.pow
    psf, psb, bkb, r_bkb = cx["psf"], cx["psb"], cx["bkb"], cx["r_bkb"]
    psw = Ring(list(cx["psw"].bufs) + list(cx["psf"].bufs))
    ident_b, r_cstb = cx["ident_b"], cx["r_cstb"]
    sm_c, r_smc, gvec, r_gvec = cx["sm_c"], cx["r_smc"], cx["gvec"], cx["r_gvec"]
    x_d, win_d, st_o, st_x1 = cx["x_d"], cx["win_d"], cx["st_o"], cx["st_x1"]
    S.begin_phase()
    gbc, r_gbc = S.buf("gbc", [128, 2, D], F32)
    S.dma(gbc[:], cx["st_gbc"], writes=[r_gbc])
    wG, _ = S.buf("wG", [128, 8, 3072], BF16)
    r_wGb = [Res("wG%d" % i) for i in range(6)]
    for i in range(6):
        S.dma(wG[:, :, i * 512:(i + 1) * 512], win_d[:, C_GA + i * 512:C_GA + (i + 1) * 512].rearrange("(k p) n -> p k n", p=128),
              writes=[r_wGb[i]], queue="pool")
    wU, r_wU = S.buf("wU", [128, 8, D], BF16)
    S.dma(wU[:, 0:4, :], cx["wua_d"].rearrange("(k p) n -> p k n", p=128), writes=[r_wU], queue="pool")
    S.dma(wU[:, 4:8, :], cx["wub_d"].rearrange("(k p) n -> p k n", p=128), writes=[r_wU], queue="pool")
    wO, r_wO = S.buf("wO", [128, 8, D], BF16)
    S.dma(wO[:], cx["wout_d"].rearrange("(k p) n -> p k n", p=128), writes=[r_wO], queue="pool")
    xring = S.ring("xt", [128, D], F32, 2)
    xnring = S.ring("xn", [128, D], F32, 1)
    st1 = S.ring("st1", [128, 4], F32, 4)
    hTs = [S.buf("hT%d" % p_, [128, 8, 512], BF16) for p_ in range(2)]
    oring = S.ring("ot", [128, D], F32, 2)
    gsr = S.ring("gs", [128, 512], F32, 2)
    f512 = S.ring("f512", [128, 512], F32, 4)
    gnr = S.ring("gn", [128, 128], F32, 3)
    gn2r = S.ring("gn2", [128, 4, 128], BF16, 2)
    st8 = S.ring("st8", [128, 16], F32, 3)
    gnT, r_gnT = S.buf("gnT", [128, 8, 512], BF16)
    sgT, r_sgT = S.buf("sgT", [128, 16, 512], BF16)
    mg, r_mg = S.buf("mg", [128, 8, 512], BF16)
    x1r = S.ring("x1", [128, D], F32, 2)
    ngroups = 8 if not os.environ.get("KLIM") else int(os.environ["KLIM"]) - 1
    def norm_group(g):
        hT_, r_hT_ = hTs[g % 2]
        for t in range(4):
            xt, r_xt = xring.next()
            xn, r_xn = xnring.next()
            S.dma(xt[:], x_d[(g * 4 + t) * 128:(g * 4 + t + 1) * 128, :], writes=[r_xt])
            _norm_hT(S, cx, xt, r_xt, xn, r_xn, st1, hT_, r_hT_, t, 0, 1, psf)

    norm_group(0)
    for g in range(ngroups):
        tile0 = g * 4
        hT, r_hT = hTs[g % 2]
        def gen_norm(tile0=tile0):
          for t in range(4):
            tl = tile0 + t
            ot, r_ot = oring.next()
            S.dma(ot[:], st_o[tl], writes=[r_ot])
            st, r_st = st8.next()
            junk, r_junk = f512.next()
            for hh in range(8):
                S.act(junk[:, 0:128], ot[:, hh * 128:(hh + 1) * 128], AF.Square, [r_ot], [r_junk, r_st], accum_out=st[:, hh:hh + 1])
            S.ts("dve", st[:, 8:16], st[:, 0:8], 1.0 / 128, EPS, MUL, ADD, [r_st], [r_st])
            S.tt("pool", st[:, 8:16], st[:, 8:16], sm_c[:, 0:1].to_broadcast([128, 8]), POW, [r_st, r_smc], [r_st])
            for m in range(2):
                pb, r_pb = psw.next()
                for k in range(8):
                    S.mm(pb[:, :], hT[:, k, t * 128:(t + 1) * 128], wG[:, k, m * 512:(m + 1) * 512], [r_hT, r_wGb[m]], [r_pb], start=(k == 0), stop=(k == 7))
                gs, r_gs = gsr.next()
                S.act(gs[:], pb[:, :], AF.Silu, [r_pb], [r_gs])
                o3 = ot[:, m * 512:(m + 1) * 512].rearrange("p (h n) -> p h n", h=4)
                g3 = gs[:].rearrange("p (h n) -> p h n", h=4)
                S.tt("dve", g3, g3, gvec[:, m, :].unsqueeze(1).to_broadcast([128, 4, 128]), MUL, [r_gs, r_gvec], [r_gs])
                S.tt("dve", g3, g3, st[:, 8 + m * 4:12 + m * 4].unsqueeze(2).to_broadcast([128, 4, 128]), MUL, [r_gs, r_st], [r_gs])
                gn2, r_gn2 = gn2r.next()
                S.tt("dve", gn2[:], g3, o3, MUL, [r_gs, r_ot], [r_gn2])
                for h in range(4):
                    S.tr(bkb[:, h * 128:(h + 1) * 128], gn2[:, h, :], ident_b, [r_gn2, r_cstb], [r_bkb])
                S.cp("act", gnT[:, m * 4:(m + 1) * 4, t * 128:(t + 1) * 128], bkb[:, 0:512].rearrange("p (h n) -> p h n", h=4), [r_bkb], [r_gnT])
                yield
        def gen_gate():
          for c in range(16):
            pb, r_pb = psw.next()
            for k in range(8):
                S.mm(pb[:, :], wG[:, k, 1024 + c * 128:1024 + (c + 1) * 128], hT[:, k, :], [r_hT, r_wGb[2 + c // 4]], [r_pb], start=(k == 0), stop=(k == 7))
            th, r_th = f512.next()
            S.act(th[:], pb[:, :], AF.Tanh, [r_pb], [r_th], scale=0.5)
            S.ts("pool" if c % 2 else "dve", sgT[:, c, :], th[:], 0.5, 0.5, MUL, ADD, [r_th], [r_sgT])
            yield
        gens_ = [gen_gate(), gen_norm()]
        while gens_:
            for g_ in list(gens_):
                try:
                    next(g_)
                except StopIteration:
                    gens_.remove(g_)
        if g + 1 < ngroups:
            norm_group(g + 1)
        for dc in range(8):
            pa, r_pa = psw.next()
            for k in range(4):
                S.mm(pa[:, :], wU[:, k, dc * 128:(dc + 1) * 128], gnT[:, k, :], [r_wU, r_gnT], [r_pa], start=(k == 0), stop=(k == 3))
            pb, r_pb = psw.next()
            for k in range(4):
                S.mm(pb[:, :], wU[:, 4 + k, dc * 128:(dc + 1) * 128], gnT[:, 4 + k, :], [r_wU, r_gnT], [r_pb], start=(k == 0), stop=(k == 3))
            t1, r_t1 = f512.next()
            S.tt("dve", t1[:], pa[:, :], sgT[:, dc, :], MUL, [r_pa, r_sgT], [r_t1])
            t2, r_t2 = f512.next()
            S.tt("dve", t2[:], pb[:, :], sgT[:, 8 + dc, :], MUL, [r_pb, r_sgT], [r_t2])
            S.tt("pool", mg[:, dc, :], t1[:], t2[:], ADD, [r_t1, r_t2], [r_mg])
        for t in range(4):
            tl = tile0 + t
            xt, r_xt = xring.next()
            S.dma(xt[:], x_d[tl * 128:(tl + 1) * 128, :], writes=[r_xt])
            x1, r_x1 = x1r.next()
            for nh in range(2):
                pb, r_pb = psw.next()
                for k in range(8):
                    S.mm(pb[:, :], mg[:, k, t * 128:(t + 1) * 128], wO[:, k, nh * 512:(nh + 1) * 512], [r_mg, r_wO], [r_pb], start=(k == 0), stop=(k == 7))
                tm, r_tm = f512.next()
                S.tt("dve", tm[:], pb[:, :], gbc[:, 0, nh * 512:(nh + 1) * 512], MUL, [r_pb, r_gbc], [r_tm])
                S.tt("pool", x1[:, nh * 512:(nh + 1) * 512], tm[:], xt[:, nh * 512:(nh + 1) * 512], ADD, [r_tm, r_xt], [r_x1])
            S.dma(st_x1[tl], x1[:], reads=[r_x1])
    S.end_phase()


def dense_c2(cx):
    import os
    S = cx["S"]
    MUL, ADD, POW = ALU.mult, ALU.add, ALU.pow
    psw, psf = cx["psw"], cx["psf"]
    sm_c, r_smc = cx["sm_c"], cx["r_smc"]
    st_x1, out_d = cx["st_x1"], cx["out_d"]
    S.begin_phase()
    gbc, r_gbc = S.buf("gbc", [128, D], F32)
    S.dma(gbc[:], cx["st_gbc"][:, 1, :], writes=[r_gbc])
    fWi, _ = S.buf("fWi", [128, 8, 2 * DFF], BF16)
    r_fWg = [Res("fWg%d" % i) for i in range(11)]
    r_fWu = [Res("fWu%d" % i) for i in range(11)]
    for i in range(11):
        for off, rr in ((0, r_fWg), (DFF, r_fWu)):
            S.dma(fWi[:, :, off + i * 256:off + (i + 1) * 256], cx["fwi_d"][:, off + i * 256:off + (i + 1) * 256].rearrange("(k p) n -> p k n", p=128),
                  writes=[rr[i]], queue="pool")
    fWo, _ = S.buf("fWo", [128, 22, D], BF16)
    r_fWoj = [Res("fWo%d" % i) for i in range(11)]
    for j0 in range(0, 22, 2):
        S.dma(fWo[:, j0:j0 + 2, :], cx["fwo_d"][j0 * 128:(j0 + 2) * 128, :].rearrange("(j p) n -> p j n", p=128), writes=[r_fWoj[j0 // 2]], queue="pool")
    fgb, r_fgb = S.buf("fgb", [128, D], F32)
    S.dma(fgb[:], cx["fng_d"].partition_broadcast(128), writes=[r_fgb])
    x1gs = [S.buf("x1g%d" % p_, [128, 2, D], F32) for p_ in range(2)]
    xnring = S.ring("xn", [128, D], F32, 1)
    st1 = S.ring("st1", [128, 4], F32, 4)
    h2Ts = [S.buf("h2T%d" % p_, [128, 8, 256], BF16) for p_ in range(2)]
    aT, r_aT = S.buf("aT", [128, 22, 256], BF16)
    f256 = S.ring("f256", [128, 256], F32, 2)
    f512 = S.ring("f512", [128, 512], F32, 2)
    x2r = S.ring("x2", [128, D], F32, 1)
    ngroups = 16 if not os.environ.get("KLIM") else 2 * (int(os.environ["KLIM"]) - 1)
    def norm_group(g):
        x1g, r_x1g = x1gs[g % 2]
        h2T, r_h2T = h2Ts[g % 2]
        for t in range(2):
            S.dma(x1g[:, t, :], st_x1[g * 2 + t], writes=[r_x1g])
            xn, r_xn = xnring.next()
            _norm_hT(S, cx, x1g[:, t, :], r_x1g, xn, r_xn, st1, h2T, r_h2T, t, 2, 3, psf)

    norm_group(0)
    for g in range(ngroups):
        tile0 = g * 2
        x1g, r_x1g = x1gs[g % 2]
        h2T, r_h2T = h2Ts[g % 2]
        for j in range(22):
            pg, r_pg = psw.next()
            for k in range(8):
                S.mm(pg[:, 0:256], fWi[:, k, j * 128:(j + 1) * 128], h2T[:, k, :], [r_fWg[j // 2], r_h2T], [r_pg], start=(k == 0), stop=(k == 7))
            pu, r_pu = psw.next()
            for k in range(8):
                S.mm(pu[:, 0:256], fWi[:, k, DFF + j * 128:DFF + (j + 1) * 128], h2T[:, k, :], [r_fWu[j // 2], r_h2T], [r_pu], start=(k == 0), stop=(k == 7))
            sg, r_sg = f256.next()
            S.act(sg[:], pg[:, 0:256], AF.Silu, [r_pg], [r_sg])
            S.tt("dve", aT[:, j, :], pu[:, 0:256], sg[:], MUL, [r_pu, r_sg], [r_aT])
        if g + 1 < ngroups:
            norm_group(g + 1)
        for t in range(2):
            tl = tile0 + t
            x2, r_x2 = x2r.next()
            for nh in range(2):
                pb, r_pb = psw.next()
                for j in range(22):
                    S.mm(pb[:, :], aT[:, j, t * 128:(t + 1) * 128], fWo[:, j, nh * 512:(nh + 1) * 512], [r_aT, r_fWoj[j // 2]], [r_pb], start=(j == 0), stop=(j == 21))
                tm, r_tm = f512.next()
                S.tt("dve", tm[:], pb[:, :], gbc[:, nh * 512:(nh + 1) * 512], MUL, [r_pb, r_gbc], [r_tm])
                S.tt("pool", x2[:, nh * 512:(nh + 1) * 512], tm[:], x1g[:, t, nh * 512:(nh + 1) * 512], ADD, [r_tm, r_x1g], [r_x2])
            ss, r_ss = st1.next()
            xn, r_xn = xnring.next()
            S.act(xn[:], x2[:], AF.Square, [r_x2], [r_xn, r_ss], accum_out=ss[:, 0:1])
            S.ts("dve", ss[:, 1:2], ss[:, 0:1], 1.0 / D, EPS, MUL, ADD, [r_ss], [r_ss])
            S.tt("pool", ss[:, 2:3], ss[:, 1:2], sm_c[:, 0:1], POW, [r_ss, r_smc], [r_ss])
            S.stt(x2[:], x2[:], ss[:, 2:3], fgb[:], MUL, MUL, [r_x2, r_ss, r_fgb], [r_x2])
            S.dma(out_d[tl * 128:(tl + 1) * 128, :], x2[:], reads=[r_x2], final=True)
    S.end_phase(last=True)
```
